# Optimizing a Trainium2 kernel written in Bass

```python
import math
import jax, jax.numpy as jnp
from jax import lax
import numpy as np

D_MODEL = 1024
BATCH = 4
SEQ = 8192
DEPTH = 4

PLE_DIM = 256
BRANCH_WIDTH = 512
N_BRANCH = 3
SSM_WIDTH = BRANCH_WIDTH
SSM_GROUP = 16
SSM_GROUPS = SSM_WIDTH // SSM_GROUP
SSM_STATE = 64
DT_MIN = 1e-3
DT_MAX = 1e-1
CONV_WIDTH = BRANCH_WIDTH
CONV_TAPS = 3
HEAD_DIM = 64
N_Q_HEADS = BRANCH_WIDTH // HEAD_DIM
N_KV_HEADS = 2
GQA_GROUP = N_Q_HEADS // N_KV_HEADS
ATTN_WIDTH = N_Q_HEADS * HEAD_DIM
KV_WIDTH = N_KV_HEADS * HEAD_DIM
WINDOW = 128
BLOCK = WINDOW
ATTN_SCALE = 1.0 / math.sqrt(HEAD_DIM)
REL_BUCKETS = 32
REL_MAX_DIST = 128
FFN_HIDDEN = -(-8 * D_MODEL // (3 * 256)) * 256
RMS_EPS = 1e-6

IN_SIZES = (SSM_WIDTH, CONV_WIDTH, CONV_WIDTH, CONV_WIDTH, ATTN_WIDTH, KV_WIDTH, KV_WIDTH, N_BRANCH * D_MODEL)
IN_WIDTH = SSM_WIDTH + 3 * CONV_WIDTH + ATTN_WIDTH + 2 * KV_WIDTH + N_BRANCH * D_MODEL

kernel_name = "hybrid_s5_shortconv_swa_gated_trunk"


def rms_norm(x, g):
    xf = x.astype(jnp.float32)
    y = xf * lax.rsqrt(jnp.mean(xf * xf, axis=-1, keepdims=True) + RMS_EPS)
    return (y * g.astype(jnp.float32)).astype(x.dtype)


def split_columns(z):
    offs, acc = [], 0
    for s in IN_SIZES[:-1]:
        acc += s
        offs.append(acc)
    return jnp.split(z, offs, axis=-1)


def t5_bucket(dist):
    exact = REL_BUCKETS // 2
    df = jnp.maximum(dist, 1).astype(jnp.float32)
    large = exact + (jnp.log(df / exact) / math.log(REL_MAX_DIST / exact) * (REL_BUCKETS - exact)).astype(jnp.int32)
    large = jnp.minimum(large, REL_BUCKETS - 1)
    return jnp.where(dist < exact, dist, large)


def band_bias_and_mask(rel_table, n_blocks):
    qi = jnp.arange(BLOCK)[:, None]
    kj = jnp.arange(2 * BLOCK)[None, :]
    dist = qi + BLOCK - kj
    band = (dist >= 0) & (dist < WINDOW)
    bucket = t5_bucket(jnp.clip(dist, 0, REL_MAX_DIST - 1))
    bias = jnp.transpose(rel_table[bucket], (2, 0, 1)).astype(jnp.float32)
    blk = jnp.arange(n_blocks)[:, None, None]
    valid = band[None] & ((blk > 0) | (kj[None] >= BLOCK))
    return bias, valid


def s5_ssm(u, lam_re, lam_im, b_re, b_im, c_re, c_im, d_skip, log_dt, w_glu):
    bsz, seq, _ = u.shape
    ug = u.reshape(bsz, seq, SSM_GROUPS, SSM_GROUP)
    dt = jnp.exp(log_dt)[:, None]
    mag = jnp.exp(lam_re * dt)
    ang = lam_im * dt
    a_re = mag * jnp.cos(ang)
    a_im = mag * jnp.sin(ang)
    den = lam_re * lam_re + lam_im * lam_im
    nr = a_re - 1.0
    coef_re = (nr * lam_re + a_im * lam_im) / den
    coef_im = (a_im * lam_re - nr * lam_im) / den
    bb_re = coef_re[..., None] * b_re - coef_im[..., None] * b_im
    bb_im = coef_re[..., None] * b_im + coef_im[..., None] * b_re
    bu_re = jnp.einsum('bsgp,gnp->bsgn', ug, bb_re)
    bu_im = jnp.einsum('bsgp,gnp->bsgn', ug, bb_im)
    a_re_t = jnp.broadcast_to(a_re[None, None], (1, seq, SSM_GROUPS, SSM_STATE))
    a_im_t = jnp.broadcast_to(a_im[None, None], (1, seq, SSM_GROUPS, SSM_STATE))

    def combine(left, right):
        a1r, a1i, b1r, b1i = left
        a2r, a2i, b2r, b2i = right
        return (a2r * a1r - a2i * a1i,
                a2r * a1i + a2i * a1r,
                a2r * b1r - a2i * b1i + b2r,
                a2r * b1i + a2i * b1r + b2i)

    _, _, h_re, h_im = lax.associative_scan(combine, (a_re_t, a_im_t, bu_re, bu_im), axis=1)
    y = jnp.einsum('gpn,bsgn->bsgp', c_re, h_re) - jnp.einsum('gpn,bsgn->bsgp', c_im, h_im)
    y = y.reshape(bsz, seq, SSM_WIDTH) + d_skip * u
    y = jax.nn.gelu(y)
    return y * jax.nn.sigmoid(y @ w_glu)


def short_conv(b_gate, c_gate, xc, conv_w):
    v = c_gate * xc
    vp = jnp.pad(v, ((0, 0), (CONV_TAPS - 1, 0), (0, 0)))
    seq = v.shape[1]
    y = conv_w[0] * vp[:, 0:seq] + conv_w[1] * vp[:, 1:seq + 1] + conv_w[2] * vp[:, 2:seq + 2]
    return b_gate * y


def swa_attention(q, k, v, sinks, bias, valid):
    bsz, seq, _ = q.shape
    nb = seq // BLOCK
    qb = q.reshape(bsz, nb, BLOCK, N_KV_HEADS, GQA_GROUP, HEAD_DIM)

    def with_prev(t):
        tb = t.reshape(bsz, nb, BLOCK, N_KV_HEADS, HEAD_DIM)
        prev = jnp.pad(tb, ((0, 0), (1, 0), (0, 0), (0, 0), (0, 0)))[:, :-1]
        return jnp.concatenate([prev, tb], axis=2)

    kb = with_prev(k)
    vb = with_prev(v)
    s = jnp.einsum('bnqhgd,bnkhd->bnhgqk', qb, kb).astype(jnp.float32) * ATTN_SCALE
    s = s + bias.reshape(N_KV_HEADS, GQA_GROUP, BLOCK, 2 * BLOCK)
    s = jnp.where(valid[None, :, None, None], s, -jnp.inf)
    sink = sinks.astype(jnp.float32).reshape(N_KV_HEADS, GQA_GROUP)[None, None, :, :, None, None]
    m = jnp.maximum(jnp.max(s, axis=-1, keepdims=True), sink)
    pexp = jnp.exp(s - m)
    w = pexp / (jnp.sum(pexp, axis=-1, keepdims=True) + jnp.exp(sink - m))
    o = jnp.einsum('bnhgqk,bnkhd->bnqhgd', w.astype(v.dtype), vb)
    return o.reshape(bsz, seq, ATTN_WIDTH)


def setup_inputs(seed: int = 0) -> dict:
    key = jax.random.key(seed)
    ks = jax.random.split(key, 26)
    f32 = jnp.float32

    def nrm(k, shape, scale):
        return jax.random.normal(k, shape, f32) * scale

    n_idx = jnp.arange(SSM_STATE, dtype=f32)
    log_dt = jax.random.uniform(ks[10], (DEPTH, SSM_GROUPS), f32, math.log(DT_MIN), math.log(DT_MAX))
    return {
        "x": nrm(ks[0], (BATCH, SEQ, D_MODEL), 1.0),
        "p": nrm(ks[1], (DEPTH, BATCH, SEQ, PLE_DIM), 1.0),
        "rel_bias": nrm(ks[2], (REL_BUCKETS, N_Q_HEADS), 0.1),
        "norm_mix": 1.0 + nrm(ks[3], (DEPTH, D_MODEL), 0.02),
        "w_in": nrm(ks[4], (DEPTH, D_MODEL, IN_WIDTH), D_MODEL ** -0.5),
        "ssm_lambda_re": -0.5 + nrm(ks[5], (DEPTH, SSM_GROUPS, SSM_STATE), 0.01),
        "ssm_lambda_im": jnp.pi * n_idx + nrm(ks[6], (DEPTH, SSM_GROUPS, SSM_STATE), 0.01),
        "ssm_b_re": nrm(ks[7], (DEPTH, SSM_GROUPS, SSM_STATE, SSM_GROUP), (2 * SSM_GROUP) ** -0.5),
        "ssm_b_im": nrm(ks[8], (DEPTH, SSM_GROUPS, SSM_STATE, SSM_GROUP), (2 * SSM_GROUP) ** -0.5),
        "ssm_c_re": nrm(ks[9], (DEPTH, SSM_GROUPS, SSM_GROUP, SSM_STATE), SSM_STATE ** -0.5),
        "ssm_c_im": nrm(ks[11], (DEPTH, SSM_GROUPS, SSM_GROUP, SSM_STATE), SSM_STATE ** -0.5),
        "ssm_d": nrm(ks[12], (DEPTH, SSM_WIDTH), 1.0),
        "ssm_log_dt": log_dt,
        "ssm_w_glu": nrm(ks[13], (DEPTH, SSM_WIDTH, SSM_WIDTH), SSM_WIDTH ** -0.5),
        "conv_w": nrm(ks[14], (DEPTH, CONV_TAPS, CONV_WIDTH), CONV_TAPS ** -0.5),
        "attn_sinks": nrm(ks[15], (DEPTH, N_Q_HEADS), 0.5),
        "w_branch": nrm(ks[16], (DEPTH, N_BRANCH, BRANCH_WIDTH, D_MODEL), BRANCH_WIDTH ** -0.5),
        "w_out": nrm(ks[17], (DEPTH, D_MODEL, D_MODEL), D_MODEL ** -0.5),
        "norm_ffn": 1.0 + nrm(ks[18], (DEPTH, D_MODEL), 0.02),
        "w_ffn_in": nrm(ks[19], (DEPTH, D_MODEL, 2 * FFN_HIDDEN), D_MODEL ** -0.5),
        "w_ffn_out": nrm(ks[20], (DEPTH, FFN_HIDDEN, D_MODEL), FFN_HIDDEN ** -0.5),
        "norm_ple": 1.0 + nrm(ks[21], (DEPTH, D_MODEL), 0.02),
        "w_ple_gate": nrm(ks[22], (DEPTH, D_MODEL, D_MODEL), D_MODEL ** -0.5),
        "w_ple_proj": nrm(ks[23], (DEPTH, PLE_DIM, D_MODEL), PLE_DIM ** -0.5),
        "norm_final": 1.0 + nrm(ks[24], (D_MODEL,), 0.02),
    }


def reference(x, p, rel_bias, norm_mix, w_in, ssm_lambda_re, ssm_lambda_im, ssm_b_re, ssm_b_im,
              ssm_c_re, ssm_c_im, ssm_d, ssm_log_dt, ssm_w_glu, conv_w, attn_sinks, w_branch, w_out,
              norm_ffn, w_ffn_in, w_ffn_out, norm_ple, w_ple_gate, w_ple_proj, norm_final):
    seq = x.shape[1]
    bias, valid = band_bias_and_mask(rel_bias, seq // BLOCK)
    for i in range(DEPTH):
        h = rms_norm(x, norm_mix[i])
        z = h @ w_in[i]
        u, cb, cc, cx, q, k, v, gates = split_columns(z)
        y_ssm = s5_ssm(u, ssm_lambda_re[i], ssm_lambda_im[i], ssm_b_re[i], ssm_b_im[i],
                       ssm_c_re[i], ssm_c_im[i], ssm_d[i], ssm_log_dt[i], ssm_w_glu[i])
        y_conv = short_conv(cb, cc, cx, conv_w[i])
        y_attn = swa_attention(q, k, v, attn_sinks[i], bias, valid)
        g = jax.nn.sigmoid(gates)
        merged = (g[..., 0:D_MODEL] * (y_ssm @ w_branch[i, 0])
                  + g[..., D_MODEL:2 * D_MODEL] * (y_conv @ w_branch[i, 1])
                  + g[..., 2 * D_MODEL:3 * D_MODEL] * (y_attn @ w_branch[i, 2]))
        x = x + merged @ w_out[i]
        hf = rms_norm(x, norm_ffn[i]) @ w_ffn_in[i]
        x = x + (jax.nn.silu(hf[..., :FFN_HIDDEN]) * hf[..., FFN_HIDDEN:]) @ w_ffn_out[i]
        pg = jax.nn.sigmoid(rms_norm(x, norm_ple[i]) @ w_ple_gate[i])
        x = x + pg * (p[i] @ w_ple_proj[i])
    return rms_norm(x, norm_final)
```

```python
import math
import numpy as np
import concourse.bass as bass
import concourse.mybir as mybir
from concourse.bass_utils import run_bass_kernel_spmd

F32 = mybir.dt.float32
BF16 = mybir.dt.bfloat16
AF = mybir.ActivationFunctionType
ALU = mybir.AluOpType
AX = mybir.AxisListType

D = 1024
KC = 8
NT = 512
BW = 512
FF = 2816
FC = 22
PLE = 256
G = 32
NS = 64
WIN = 5888
ATT_SCALE = 1.0 / 8.0
EPS = 1e-6
ERA = 30000
SLOT = 4096
HF = 11


class Res:
    __slots__ = ("name", "w", "r", "sem", "semcnt", "excl")

    def __init__(self, name, sem=None, excl=False):
        self.name = name
        self.excl = excl
        self.w = None
        self.r = []
        self.sem = sem
        self.semcnt = 0


class Sched:
    def __init__(self, nc):
        self.nc = nc
        self.E = {"pe": nc.tensor, "dve": nc.vector, "act": nc.scalar, "pool": nc.gpsimd, "sp": nc.sync}
        self.nsem = 0
        self.dl = []
        self.sem = {k: self.newsem() for k in self.E}
        self.cnt = {k: 0 for k in self.E}
        self.seen = {k: {} for k in self.E}
        self.ninst = 0

    def newsem(self):
        self.nsem += 1
        return self.nc.alloc_semaphore("sm%d" % self.nsem)

    def dres(self, name):
        r = Res(name, self.newsem())
        self.dl.append(r)
        return r

    def barrier(self, skip=()):
        skip = set(id(x) for x in skip)
        for e, eng in self.E.items():
            seen = self.seen[e]
            for o in self.E:
                if o != e and self.cnt[o] > 0 and seen.get(self.sem[o].name, 0) < self.cnt[o]:
                    eng.wait_ge(self.sem[o], self.cnt[o])
                    seen[self.sem[o].name] = self.cnt[o]
            for r in self.dl:
                if id(r) in skip:
                    continue
                if r.semcnt > 0 and seen.get(r.sem.name, 0) < r.semcnt:
                    eng.wait_ge(r.sem, r.semcnt)
                    seen[r.sem.name] = r.semcnt

    def _wait(self, e, R, W):
        eng = self.E[e]
        deps = []
        for r in R:
            if r.w is not None:
                deps.append(r.w + (True,))
        for w in W:
            if w.w is not None:
                deps.append(w.w + (False,))
            for x in w.r:
                deps.append(x + (False,))
        seen = self.seen[e]
        for (prod, sem, val, raw) in deps:
            if prod == e and not raw:
                continue
            if prod == "dma":
                val = sem[1].semcnt
                semh = sem[0]
            else:
                semh = sem
            key = semh.name
            if seen.get(key, 0) >= val:
                continue
            eng.wait_ge(semh, val)
            self.ninst += 1
            seen[key] = val

    def op(self, e, fn, R=(), W=()):
        W = list(W) + [r for r in R if r.excl]
        R = [r for r in R if not r.excl]
        self._wait(e, R, W)
        if self.cnt[e] >= ERA:
            self.sem[e] = self.newsem()
            self.cnt[e] = 0
        ins = fn(self.E[e])
        self.cnt[e] += 1
        ins.then_inc(self.sem[e], 1)
        self.ninst += 1
        tag = (e, self.sem[e], self.cnt[e])
        for r in R:
            r.r.append(tag)
        for w in W:
            w.w = tag
            w.r = []
        return ins

    def dma(self, q, out, in_, R=(), W=(), semres=None):
        sr = semres if semres is not None else W[0]
        self._wait(q, R, W)
        ins = self.E[q].dma_start(out=out, in_=in_)
        sr.semcnt += 16
        ins.then_inc(sr.sem, 16)
        self.ninst += 1
        tag = ("dma", (sr.sem, sr), sr.semcnt)
        for r in R:
            r.r.append(tag)
        for w in W:
            w.w = tag
            w.r = []
        return ins


def build(T, DEPTH, debug=None):
    nc = bass.Bass("TRN2", target_bir_lowering=False)
    S = Sched(nc)
    ntiles = T // NT
    dbg = {}

    def din(name, shape, dt=F32):
        return nc.dram_tensor(name, list(shape), dt, kind="ExternalInput").ap()

    def dscr(name, shape, dt=BF16):
        return nc.dram_tensor(name, list(shape), dt, kind="Internal").ap()

    def sb(name, shape, dt=F32):
        return nc.alloc_sbuf_tensor(name, list(shape), dt).ap()

    xT_d = din("xT", [D, T])
    pT_d = din("pT", [DEPTH, PLE, T])
    outT_d = nc.dram_tensor("outT", [D, T], F32, kind="ExternalOutput").ap()
    wnames = {"w_in": (D, WIN), "w_glu": (BW, BW), "w_br": (BW, 3 * D), "w_out": (D, D),
              "w_fi": (D, 2 * FF), "w_fo": (FF, D), "w_pg": (D, D), "w_pp": (PLE, D)}
    w_f = {k: din(k, [DEPTH, v[0], v[1]]) for k, v in wnames.items()}
    w_b = {k: dscr(k + "_b", [DEPTH, v[0], v[1]]) for k, v in wnames.items()}
    w_res = {(k, l): S.dres("wr_%s%d" % (k, l)) for k in wnames for l in range(DEPTH)}
    gains_d = din("gains", [128, (3 * DEPTH + 1) * KC])
    convw_d = din("convw", [128, DEPTH * 4 * 3])
    dskip_d = din("dskip", [128, DEPTH * 4])
    sink_d = din("sinkb", [128, DEPTH * 8])
    bias_d = din("biasmask", [128, 8 * 256])
    identf_d = din("identf", [128, 128])
    lamP_d = din("lamP", [128, 3 * DEPTH * 16])
    lamR_d = din("lamR", [DEPTH, 3, 128, G * NS])
    bpad_d = din("bpad", [DEPTH, 2, 128, 16 * 128])
    cpad_d = din("cpad", [DEPTH, 2, 128, 16 * 128])
    sc_bb = dscr("sc_bb", [DEPTH, 128, 2 * 16 * 128])
    sc_cc = dscr("sc_cc", [DEPTH, 128, 2 * 16 * 128])
    sc_rot = dscr("sc_rot", [DEPTH, 128, 16 * 2 * NT], F32)

    xT = sb("xT_s", [128, KC, NT]); xR = [S.dres("x%d" % c) for c in range(KC)]
    hT = sb("hT_s", [128, KC, NT], BF16); hR = [Res("h%d" % c) for c in range(KC)]
    sq = sb("sq_s", [128, 2, NT], BF16); sqR = [Res("sq0"), Res("sq1")]
    rstd = sb("rstd_s", [128, NT]); rstdR = Res("rstd")
    ones_b = sb("ones_b", [128, 128], BF16); constR = Res("const")
    identf = sb("identf_s", [128, 128]); identb = sb("identb_s", [128, 128], BF16)
    gains = sb("gains_s", [128, (3 * DEPTH + 1) * KC])
    convw = sb("convw_s", [128, DEPTH * 12])
    dskip = sb("dskip_s", [128, DEPTH * 4])
    sinkb = sb("sink_s", [128, DEPTH * 8])
    biasm = sb("bias_s", [128, 8, 256])
    epsb = sb("eps_s", [128, 1])
    smallR = S.dres("small")
    uT = sb("uT_s", [128, 4, NT], BF16); uR = [Res("u%d" % c) for c in range(4)]
    ysb = sb("ysb_s", [128, NT]); ysbR = Res("ysb")
    ygT = sb("ygT_s", [128, 4, NT], BF16); ygR = [Res("yg%d" % c) for c in range(4)]
    qz = sb("qz_s", [128, 8, NT], BF16); qzR = [Res("qz%d" % h) for h in range(8)]
    kT = sb("kT_s", [128, DEPTH, NT + 128], BF16); kR = [Res("k%d" % l) for l in range(DEPTH)]
    vz = sb("vz_s", [128, DEPTH, 5 * 2 * 128], BF16); vR = [Res("v%d" % l) for l in range(DEPTH)]
    vcv = sb("vcv_s", [128, 2, NT + 2]); vcvR = [Res("vcv0"), Res("vcv1")]
    vhist = sb("vhist_s", [128, DEPTH * 4 * 2]); vhR = Res("vhist")
    acc = sb("acc_s", [128, 2, NT]); accR = [Res("acc0"), Res("acc1")]
    tmpA = sb("tmpA_s", [128, 2, NT]); tmpAR = [Res("tA0"), Res("tA1")]
    tmpB = sb("tmpB_s", [128, 2, NT]); tmpBR = [S.dres("tB0"), S.dres("tB1")]
    pTb = sb("pTb_s", [128, 2, NT], BF16); pTbR = S.dres("pTb")
    ARENA = 14336
    arena = sb("arena", [128, ARENA])
    NSLOT = 4
    wsl = [arena[:, i * 2048:(i + 1) * 2048].bitcast(BF16) for i in range(NSLOT)]
    wslR = [S.dres("wsl%d" % i) for i in range(NSLOT)]
    gT = arena[:, 8192:8192 + 2816].bitcast(BF16).rearrange("p (c t) -> p c t", t=NT)
    gR = [Res("g%d" % c) for c in range(HF)]
    mergedT = gT; mgR = gR
    ybr = [arena[:, 11008 + r * 1024:11008 + (r + 1) * 1024].bitcast(BF16).rearrange("p (c t) -> p c t", t=NT) for r in range(3)]
    ybrR = [[Res("ybr%d_%d" % (r, c)) for c in range(4)] for r in range(3)]
    bbS = sb("bbS", [128, 2, 16, 128], BF16); ccS = sb("ccS", [128, 2, 16, 128], BF16)
    bbR = S.dres("bbS"); ccR = S.dres("ccS")
    rot = sb("rot_s", [128, 2, 2, NT]); rotR = [S.dres("rot0"), S.dres("rot1")]
    rho = sb("rho_s", [128, DEPTH * 16])
    carry = sb("carry_s", [128, DEPTH * 16 * 2]); carryR = Res("carry")
    cst = sb("cst_s", [128, 4]); cstR = Res("cst")
    bre = sb("bre_s", [128, NT]); breR = Res("bre")
    bim = sb("bim_s", [128, NT]); bimR = Res("bim")
    t1 = sb("t1_s", [128, NT]); t1R = Res("t1")
    t2 = sb("t2_s", [128, NT]); t2R = Res("t2")
    t3 = sb("t3_s", [128, NT]); t3R = Res("t3")
    t4 = sb("t4_s", [128, NT]); t4R = Res("t4")
    gre = sb("gre_s", [128, NT]); greR = Res("gre")
    gim = sb("gim_s", [128, NT]); gimR = Res("gim")
    hre = sb("hre_s", [128, 2, NT], BF16); hreR = [Res("hre0"), Res("hre1")]
    nhi = sb("nhi_s", [128, 2, NT], BF16); nhiR = [Res("nhi0"), Res("nhi1")]
    s_sb = sb("ssb_s", [128, 2, 256]); ssbR = [Res("ssb0"), Res("ssb1")]
    p_f = sb("pf_s", [128, 2, 256]); pfR = [Res("pf0"), Res("pf1")]
    p_b = sb("pb_s", [128, 2, 256], BF16); pbR = [Res("pb0"), Res("pb1")]
    pTs = sb("pTs_s", [128, 2, 256], BF16); pTsR = [Res("pTs0"), Res("pTs1")]
    stat = sb("stat_s", [128, 2, 8]); statR = [Res("st0"), Res("st1")]
    NPS = 4
    psb = [nc.alloc_psum_tensor("ps%d" % i, [128, 512], F32).ap() for i in range(NPS)]
    psR = [Res("ps%d" % i, excl=True) for i in range(NPS)]
    psy = nc.alloc_psum_tensor("psy", [128, 512], F32).ap(); psyR = Res("psy", excl=True)
    pss = [nc.alloc_psum_tensor("pss%d" % i, [128, 512], F32).ap() for i in range(2)]
    pssR = [Res("pss0", excl=True), Res("pss1", excl=True)]
    pst_t = nc.alloc_psum_tensor("pst", [128, 1024], BF16).ap()
    pst = [pst_t[:, 0:256], pst_t[:, 0:256]]
    pstR = [Res("pst0", excl=True)] * 2
    st = {"ps": 0, "slot": 0, "ev": 0}

    def psum():
        i = st["ps"]; st["ps"] = (i + 1) % NPS
        return psb[i], psR[i]

    def evac_eng():
        st["ev"] ^= 1
        return "act" if st["ev"] else "dve"

    def copy(e, out, in_, R, W):
        if e == "act":
            return S.op("act", lambda g: g.activation(out=out, in_=in_, func=AF.Copy), R, W)
        return S.op(e, lambda g: g.tensor_copy(out=out, in_=in_), R, W)

    def dump(name, ap, R, shape, dt=F32):
        if debug is None or name not in debug:
            return
        d = nc.dram_tensor("dbg_" + name, list(shape), dt, kind="ExternalOutput").ap()
        r = S.dres("dbg_" + name)
        S.dma("sp", d, ap, R=R, W=[r])
        dbg[name] = r

    def tt(e, out, a, b, op, R, W):
        return S.op(e, lambda g: g.tensor_tensor(out=out, in0=a, in1=b, op=op), R, W)

    def ts(e, out, a, s1, s2, op0, op1, R, W):
        return S.op(e, lambda g: g.tensor_scalar(out=out, in0=a, scalar1=s1, scalar2=s2, op0=op0, op1=op1), R, W)

    def act(out, in_, func, R, W, bias=None, scale=None, accum=None):
        kw = {}
        if bias is not None:
            kw["bias"] = bias
        if scale is not None:
            kw["scale"] = scale
        if accum is not None:
            kw["accum_out"] = accum
        return S.op("act", lambda g: g.activation(out=out, in_=in_, func=func, **kw), R, W)

    for (dst, src) in [(gains, gains_d), (convw, convw_d), (dskip, dskip_d), (sinkb, sink_d),
                       (biasm.rearrange("p h j -> p (h j)"), bias_d), (identf, identf_d)]:
        S.dma("sp", dst, src, W=[smallR])
    S.op("dve", lambda g: g.memset(ones_b, 1.0), W=[constR])
    S.op("dve", lambda g: g.memset(epsb, EPS), W=[constR])
    S.op("dve", lambda g: g.tensor_copy(out=identb, in_=identf), R=[smallR], W=[constR])
    S.op("pool", lambda g: g.memset(qz.rearrange("p h t -> p (h t)"), 0.0), W=qzR)
    S.op("pool", lambda g: g.memset(kT.rearrange("p l t -> p (l t)"), 0.0), W=kR)
    S.op("pool", lambda g: g.memset(vz.rearrange("p l t -> p (l t)"), 0.0), W=vR)
    S.op("pool", lambda g: g.memset(vhist, 0.0), W=[vhR])
    S.op("pool", lambda g: g.memset(carry, 0.0), W=[carryR])

    for l in range(DEPTH):
        for k, (K, N) in wnames.items():
            rows = max(128, (1 << 20) // N // 128 * 128)
            r0 = 0
            while r0 < K:
                r1 = min(K, r0 + rows)
                S.dma("pool", w_b[k][l, r0:r1, :], w_f[k][l, r0:r1, :], W=[w_res[(k, l)]])
                r0 = r1
        r2 = S.dres("ccscr%d" % l)
        S.dma("pool", sc_cc[l].rearrange("p (c n) -> c p n", c=2), cpad_d[l], W=[r2])
        w_res[("cc", l)] = r2

    lamP = sb("lamP_s", [128, 3, DEPTH * 16]); lamPR = S.dres("lamP")
    S.dma("sp", lamP.rearrange("p a b -> p (a b)"), lamP_d, W=[lamPR])
    QN = 512
    LR = arena[:, 0:1536].rearrange("p (a n) -> p a n", a=3); LRR = S.dres("LR")
    wk = [arena[:, 1536 + i * 512:1536 + (i + 1) * 512] for i in range(8)]
    wkR = [Res("wk%d" % i) for i in range(8)]
    bpS = arena[:, 5632:6656].rearrange("p (a n) -> p a n", a=2); bpR = S.dres("bpS")
    bbo = arena[:, 6656:7168].bitcast(BF16).rearrange("p (a n) -> p a n", a=2); bboR = S.dres("bbo")
    rt = arena[:, 7168:11264].rearrange("p (a b c) -> p a b c", a=4, b=2); rtR = S.dres("rt")
    rtm = [arena[:, 11264 + i * 1024:11264 + (i + 1) * 1024].rearrange("p (a n) -> p a n", a=4) for i in range(3)]
    rtmR = [Res("rtm%d" % i) for i in range(3)]

    def cexp_unit(ang, c_out, s_out, tmp, RA, Rc, Rs, Rt, n_sq=3):
        sc = 1.0 / (1 << n_sq)
        act(s_out, ang, AF.Sin, [RA], [Rs], scale=sc)
        act(tmp, ang, AF.Sin, [RA], [Rt], scale=sc * 0.5)
        tt("dve", tmp, tmp, tmp, ALU.mult, [Rt], [Rt])
        ts("dve", c_out, tmp, -2.0, 1.0, ALU.mult, ALU.add, [Rt], [Rc])
        for _ in range(n_sq):
            tt("dve", tmp, c_out, s_out, ALU.mult, [Rc, Rs], [Rt])
            tt("dve", c_out, c_out, c_out, ALU.mult, [Rc], [Rc])
            tt("dve", s_out, s_out, s_out, ALU.mult, [Rs], [Rs])
            tt("dve", c_out, c_out, s_out, ALU.subtract, [Rc, Rs], [Rc])
            ts("dve", s_out, tmp, 2.0, None, ALU.mult, ALU.bypass, [Rt], [Rs])

    NL = DEPTH * 16
    pw = [sb("pw%d" % i, [128, NL]) for i in range(6)]
    pwR = [Res("pw%d" % i) for i in range(6)]
    rhoR = Res("rho")
    act(pw[0], lamP[:, 2, :], AF.Exp, [lamPR], [pwR[0]])
    tt("dve", pw[1], lamP[:, 0, :], pw[0], ALU.mult, [lamPR, pwR[0]], [pwR[1]])
    act(rho, pw[1], AF.Exp, [pwR[1]], [rhoR])
    tt("dve", pw[2], lamP[:, 1, :], pw[0], ALU.mult, [lamPR, pwR[0]], [pwR[2]])
    cexp_unit(pw[2], pw[3], pw[4], pw[5], pwR[2], pwR[3], pwR[4], pwR[5])
    for l in range(DEPTH):
        rotres_l = S.dres("rotscr%d" % l)
        w_res[("rot", l)] = rotres_l
        for gq in range(4):
            cs = pw[3][:, l * 16 + gq * 4:l * 16 + gq * 4 + 4]
            sn = pw[4][:, l * 16 + gq * 4:l * 16 + gq * 4 + 4]
            S.op("dve", lambda g: g.tensor_copy(out=rt[:, :, 0, 0], in_=cs), [pwR[3]], [rtR])
            ts("dve", rt[:, :, 1, 0], sn, -1.0, None, ALU.mult, ALU.bypass, [pwR[4]], [rtR])
            n = 1
            while n < NT:
                fc = rt[:, :, 0, 0:n]; fs = rt[:, :, 1, 0:n]
                mc = rt[:, :, 0, n - 1:n].to_broadcast([128, 4, n]); ms = rt[:, :, 1, n - 1:n].to_broadcast([128, 4, n])
                a0 = rtm[0][:, :, 0:n]; a1 = rtm[1][:, :, 0:n]; a2 = rtm[2][:, :, 0:n]
                tt("dve", a0, fc, mc, ALU.mult, [rtR], [rtmR[0]])
                tt("dve", a1, fs, ms, ALU.mult, [rtR], [rtmR[1]])
                tt("dve", a2, fc, ms, ALU.mult, [rtR], [rtmR[2]])
                tt("dve", rt[:, :, 0, n:2 * n], a0, a1, ALU.subtract, [rtmR[0], rtmR[1]], [rtR])
                tt("dve", a0, fs, mc, ALU.mult, [rtR], [rtmR[0]])
                tt("dve", rt[:, :, 1, n:2 * n], a2, a0, ALU.add, [rtmR[2], rtmR[0]], [rtR])
                n *= 2
            S.dma("sp", sc_rot[l, :, gq * 4 * 2 * NT:(gq + 1) * 4 * 2 * NT], rt.rearrange("p a b c -> p (a b c)"),
                  R=[rtR], W=[rotres_l])
        r1 = S.dres("bbscr%d" % l)
        w_res[("bb", l)] = r1
        for qd in range(4):
            for a in range(3):
                S.dma("sp", LR[:, a, :], lamR_d[l, a, :, qd * QN:(qd + 1) * QN], W=[LRR])
            for c in range(2):
                S.dma("sp", bpS[:, c, :], bpad_d[l, c, :, qd * QN:(qd + 1) * QN], W=[bpR])
            lre = LR[:, 0, :]; lim = LR[:, 1, :]
            act(wk[0], LR[:, 2, :], AF.Exp, [LRR], [wkR[0]])
            tt("dve", wk[1], lre, wk[0], ALU.mult, [LRR, wkR[0]], [wkR[1]])
            act(wk[1], wk[1], AF.Exp, [wkR[1]], [wkR[1]])
            tt("dve", wk[2], lim, wk[0], ALU.mult, [LRR, wkR[0]], [wkR[2]])
            cexp_unit(wk[2], wk[3], wk[4], wk[5], wkR[2], wkR[3], wkR[4], wkR[5])
            tt("dve", wk[3], wk[3], wk[1], ALU.mult, [wkR[3], wkR[1]], [wkR[3]])
            tt("dve", wk[4], wk[4], wk[1], ALU.mult, [wkR[4], wkR[1]], [wkR[4]])
            ts("dve", wk[3], wk[3], -1.0, None, ALU.add, ALU.bypass, [wkR[3]], [wkR[3]])
            tt("dve", wk[0], lre, lre, ALU.mult, [LRR], [wkR[0]])
            tt("dve", wk[1], lim, lim, ALU.mult, [LRR], [wkR[1]])
            tt("dve", wk[0], wk[0], wk[1], ALU.add, [wkR[0], wkR[1]], [wkR[0]])
            S.op("dve", lambda g: g.reciprocal(out=wk[0], in_=wk[0]), [wkR[0]], [wkR[0]])
            tt("dve", wk[1], wk[3], lre, ALU.mult, [wkR[3], LRR], [wkR[1]])
            tt("dve", wk[2], wk[4], lim, ALU.mult, [wkR[4], LRR], [wkR[2]])
            tt("dve", wk[1], wk[1], wk[2], ALU.add, [wkR[1], wkR[2]], [wkR[1]])
            tt("dve", wk[1], wk[1], wk[0], ALU.mult, [wkR[1], wkR[0]], [wkR[1]])
            tt("dve", wk[2], wk[4], lre, ALU.mult, [wkR[4], LRR], [wkR[2]])
            tt("dve", wk[5], wk[3], lim, ALU.mult, [wkR[3], LRR], [wkR[5]])
            tt("dve", wk[2], wk[2], wk[5], ALU.subtract, [wkR[2], wkR[5]], [wkR[2]])
            tt("dve", wk[2], wk[2], wk[0], ALU.mult, [wkR[2], wkR[0]], [wkR[2]])
            tt("dve", wk[3], wk[1], bpS[:, 0, :], ALU.mult, [wkR[1], bpR], [wkR[3]])
            tt("dve", wk[4], wk[2], bpS[:, 1, :], ALU.mult, [wkR[2], bpR], [wkR[4]])
            tt("dve", bbo[:, 0, :], wk[3], wk[4], ALU.subtract, [wkR[3], wkR[4]], [bboR])
            tt("dve", wk[3], wk[1], bpS[:, 1, :], ALU.mult, [wkR[1], bpR], [wkR[3]])
            tt("dve", wk[4], wk[2], bpS[:, 0, :], ALU.mult, [wkR[2], bpR], [wkR[4]])
            tt("dve", bbo[:, 1, :], wk[3], wk[4], ALU.add, [wkR[3], wkR[4]], [bboR])
            dst = sc_bb[l].rearrange("p (c n) -> p c n", c=2)[:, :, qd * QN:(qd + 1) * QN]
            S.dma("sp", dst, bbo, R=[bboR], W=[r1])
    S.barrier(skip=[w_res[(k, l)] for k in wnames for l in range(DEPTH)])

    def wload(k, l, r0, nkc, c0, ncols):
        i = st["slot"]; st["slot"] = (i + 1) % NSLOT
        assert nkc * ncols <= SLOT
        src = w_b[k][l, r0:r0 + nkc * 128, c0:c0 + ncols].rearrange("(kc p) n -> p kc n", p=128)
        dst = wsl[i][:, 0:nkc * ncols].rearrange("p (kc n) -> p kc n", n=ncols)
        S.dma("sp", dst, src, R=[w_res[(k, l)]], W=[wslR[i]])
        return dst, wslR[i]

    def group(out_ap, outR, lhs_list, rhs_list, R):
        n = len(lhs_list)
        for i in range(n):
            S.op("pe", lambda g, i=i: g.matmul(out_ap, lhsT=lhs_list[i], rhs=rhs_list[i], start=(i == 0), stop=(i == n - 1)),
                 R, [outR])

    def norm():
        ps, pR = psum()
        for c in range(KC):
            j = c % 2
            act(sq[:, j, :], xT[:, c, :], AF.Square, [xR[c]], [sqR[j]])
            S.op("pe", lambda g, c=c, j=j: g.matmul(ps, lhsT=ones_b, rhs=sq[:, j, :], start=(c == 0), stop=(c == KC - 1)),
                 [constR, sqR[j]], [pR])
        act(rstd, ps, AF.Sqrt, [pR, constR], [rstdR], bias=epsb[:, 0:1], scale=1.0 / D)
        S.op("dve", lambda g: g.reciprocal(out=rstd, in_=rstd), [rstdR], [rstdR])

    def norm_apply(gidx):
        for c in range(KC):
            S.op("dve", lambda g, c=c: g.scalar_tensor_tensor(
                out=hT[:, c, :], in0=xT[:, c, :], scalar=gains[:, gidx * KC + c:gidx * KC + c + 1], in1=rstd,
                op0=ALU.mult, op1=ALU.mult), [xR[c], rstdR, smallR], [hR[c]])

    hrhs = [hT[:, c, :] for c in range(KC)]

    for ti in range(ntiles):
        t0 = ti * NT
        for c in range(KC):
            S.dma("sp", xT[:, c, :], xT_d[c * 128:(c + 1) * 128, t0:t0 + NT], W=[xR[c]])
        for l in range(DEPTH):
            d0 = (ti == 0 and l == 0)
            norm()
            norm_apply(0 * DEPTH + l)
            if d0:
                dump("h0", hT.rearrange("p c t -> p (c t)"), hR, [128, KC * NT], BF16)
            S.dma("pool", bbS.rearrange("p a b c -> p (a b c)"), sc_bb[l], R=[w_res[("bb", l)]], W=[bbR])
            S.dma("pool", ccS.rearrange("p a b c -> p (a b c)"), sc_cc[l], R=[w_res[("cc", l)]], W=[ccR])
            wv, wR_ = wload("w_in", l, 0, KC, 0, 512)
            for mo in range(4):
                ps, pR = psum()
                group(ps, pR, [wv[:, kc, mo * 128:(mo + 1) * 128] for kc in range(KC)], hrhs, [wR_] + hR)
                copy(evac_eng(), uT[:, mo, :], ps, [pR], [uR[mo]])
            for gp in range(16):
                ch = gp // 4
                j = gp % 2
                S.dma("pool", rot[:, j].rearrange("p a t -> p (a t)"), sc_rot[l, :, gp * 2 * NT:(gp + 1) * 2 * NT],
                      R=[w_res[("rot", l)]], W=[rotR[j]])
                Fc = rot[:, j, 0, :]; Fs = rot[:, j, 1, :]
                ps_r, pRr = psum()
                S.op("pe", lambda g: g.matmul(ps_r, lhsT=bbS[:, 0, gp, :], rhs=uT[:, ch, :], start=True, stop=True),
                     [bbR, uR[ch]], [pRr])
                ps_i, pRi = psum()
                S.op("pe", lambda g: g.matmul(ps_i, lhsT=bbS[:, 1, gp, :], rhs=uT[:, ch, :], start=True, stop=True),
                     [bbR, uR[ch]], [pRi])
                copy("act", bre, ps_r, [pRr], [breR])
                copy("act", bim, ps_i, [pRi], [bimR])
                tt("dve", t1, bre, Fc, ALU.mult, [breR, rotR[j]], [t1R])
                tt("pool", t2, bim, Fs, ALU.mult, [bimR, rotR[j]], [t2R])
                tt("dve", gre, t1, t2, ALU.subtract, [t1R, t2R], [greR])
                tt("pool", t3, bre, Fs, ALU.mult, [breR, rotR[j]], [t3R])
                tt("dve", t4, bim, Fc, ALU.mult, [bimR, rotR[j]], [t4R])
                tt("dve", gim, t3, t4, ALU.add, [t3R, t4R], [gimR])
                rb = rho[:, l * 16 + gp:l * 16 + gp + 1].to_broadcast([128, NT])
                ci = (l * 16 + gp) * 2
                S.op("dve", lambda g: g.tensor_tensor_scan(out=t1, data0=rb, data1=gre, initial=carry[:, ci:ci + 1],
                                                           op0=ALU.mult, op1=ALU.add), [greR, rhoR, carryR], [t1R])
                S.op("dve", lambda g: g.tensor_tensor_scan(out=t2, data0=rb, data1=gim, initial=carry[:, ci + 1:ci + 2],
                                                           op0=ALU.mult, op1=ALU.add), [gimR, rhoR, carryR], [t2R])
                L1 = slice(NT - 1, NT)
                tt("pool", cst[:, 0:1], t1[:, L1], Fc[:, L1], ALU.mult, [t1R, rotR[j]], [cstR])
                tt("pool", cst[:, 1:2], t2[:, L1], Fs[:, L1], ALU.mult, [t2R, rotR[j]], [cstR])
                tt("pool", cst[:, 2:3], t2[:, L1], Fc[:, L1], ALU.mult, [t2R, rotR[j]], [cstR])
                tt("pool", cst[:, 3:4], t1[:, L1], Fs[:, L1], ALU.mult, [t1R, rotR[j]], [cstR])
                tt("pool", carry[:, ci:ci + 1], cst[:, 0:1], cst[:, 1:2], ALU.add, [cstR], [carryR])
                tt("pool", carry[:, ci + 1:ci + 2], cst[:, 2:3], cst[:, 3:4], ALU.subtract, [cstR], [carryR])
                tt("dve", t3, t1, Fc, ALU.mult, [t1R, rotR[j]], [t3R])
                tt("pool", t4, t2, Fs, ALU.mult, [t2R, rotR[j]], [t4R])
                tt("dve", hre[:, j, :], t3, t4, ALU.add, [t3R, t4R], [hreR[j]])
                tt("pool", gre, t1, Fs, ALU.mult, [t1R, rotR[j]], [greR])
                tt("dve", gim, t2, Fc, ALU.mult, [t2R, rotR[j]], [gimR])
                tt("dve", nhi[:, j, :], gre, gim, ALU.subtract, [greR, gimR], [nhiR[j]])
                S.op("pe", lambda g: g.matmul(psy, lhsT=ccS[:, 0, gp, :], rhs=hre[:, j, :], start=(gp % 4 == 0), stop=False),
                     [ccR, hreR[j]], [psyR])
                S.op("pe", lambda g: g.matmul(psy, lhsT=ccS[:, 1, gp, :], rhs=nhi[:, j, :], start=False, stop=(gp % 4 == 3)),
                     [ccR, nhiR[j]], [psyR])
                if gp % 4 == 3:
                    S.op("dve", lambda g: g.scalar_tensor_tensor(out=ysb, in0=uT[:, ch, :], scalar=dskip[:, l * 4 + ch:l * 4 + ch + 1],
                                                                 in1=psy, op0=ALU.mult, op1=ALU.add), [uR[ch], psyR, smallR], [ysbR])
                    if d0 and ch == 0:
                        dump("yssm0", ysb, [ysbR], [128, NT])
                    act(t3, ysb, AF.Square, [ysbR], [t3R])
                    ts("dve", t3, t3, 0.044715, 1.0, ALU.mult, ALU.add, [t3R], [t3R])
                    tt("pool", t3, t3, ysb, ALU.mult, [t3R, ysbR], [t3R])
                    act(t4, t3, AF.Sigmoid, [t3R], [t4R], scale=1.5957691216057308)
                    tt("dve", ygT[:, ch, :], ysb, t4, ALU.mult, [ysbR, t4R], [ygR[ch]])
            wv, wR_ = wload("w_glu", l, 0, 4, 0, 512)
            for mo in range(4):
                ps, pR = psum()
                group(ps, pR, [wv[:, kc, mo * 128:(mo + 1) * 128] for kc in range(4)], [ygT[:, kc, :] for kc in range(4)], [wR_] + ygR)
                j = mo % 2
                act(tmpA[:, j, :], ps, AF.Sigmoid, [pR], [tmpAR[j]])
                tt("dve", ybr[0][:, mo, :], ygT[:, mo, :], tmpA[:, j, :], ALU.mult, [ygR[mo], tmpAR[j]], [ybrR[0][mo]])
            for c in range(4):
                wv, wR_ = wload("w_in", l, 0, KC, 512 + c * 384, 384)
                j = c % 2
                psB, pRB = psum(); psC, pRC = psum(); psX, pRX = psum()
                group(psC, pRC, [wv[:, kc, 128:256] for kc in range(KC)], hrhs, [wR_] + hR)
                group(psX, pRX, [wv[:, kc, 256:384] for kc in range(KC)], hrhs, [wR_] + hR)
                group(psB, pRB, [wv[:, kc, 0:128] for kc in range(KC)], hrhs, [wR_] + hR)
                copy("act", tmpA[:, j, :], psC, [pRC], [tmpAR[j]])
                hi = (l * 4 + c) * 2
                copy("pool", vcv[:, j, 0:2], vhist[:, hi:hi + 2], [vhR], [vcvR[j]])
                tt("dve", vcv[:, j, 2:NT + 2], tmpA[:, j, :], psX, ALU.mult, [tmpAR[j], pRX], [vcvR[j]])
                copy("pool", vhist[:, hi:hi + 2], vcv[:, j, NT:NT + 2], [vcvR[j]], [vhR])
                wi = (l * 4 + c) * 3
                act(acc[:, j, :], vcv[:, j, 0:NT], AF.Copy, [vcvR[j], smallR], [accR[j]], scale=convw[:, wi:wi + 1])
                S.op("dve", lambda g: g.scalar_tensor_tensor(out=acc[:, j, :], in0=vcv[:, j, 1:NT + 1], scalar=convw[:, wi + 1:wi + 2],
                                                             in1=acc[:, j, :], op0=ALU.mult, op1=ALU.add), [vcvR[j], accR[j], smallR], [accR[j]])
                S.op("dve", lambda g: g.scalar_tensor_tensor(out=acc[:, j, :], in0=vcv[:, j, 2:NT + 2], scalar=convw[:, wi + 2:wi + 3],
                                                             in1=acc[:, j, :], op0=ALU.mult, op1=ALU.add), [vcvR[j], accR[j], smallR], [accR[j]])
                tt("dve", ybr[1][:, c, :], acc[:, j, :], psB, ALU.mult, [accR[j], pRB], [ybrR[1][c]])
            wv, wR_ = wload("w_in", l, 0, KC, 2048, 512)
            for c in range(4):
                ps, pR = psum()
                group(ps, pR, [wv[:, kc, c * 128:(c + 1) * 128] for kc in range(KC)], hrhs, [wR_] + hR)
                copy("act", qz[0:64, c, :], ps[0:64, :], [pR], [qzR[c]])
                copy("dve", qz[64:128, c + 4, :], ps[64:128, :], [pR], [qzR[c + 4]])
            wv, wR_ = wload("w_in", l, 0, KC, 2560, 256)
            ps, pR = psum()
            group(ps, pR, [wv[:, kc, 0:128] for kc in range(KC)], hrhs, [wR_] + hR)
            copy("act", kT[:, l, 128:128 + NT], ps, [pR], [kR[l]])
            ps, pR = psum()
            for blk in range(4):
                group(ps[:, blk * 128:(blk + 1) * 128], pR, [hT[:, kc, blk * 128:(blk + 1) * 128] for kc in range(KC)],
                      [wv[:, kc, 128:256] for kc in range(KC)], [wR_] + hR)
            vzl = vz[:, l, :].rearrange("p (b k d) -> p b k d", b=5, k=2)
            psv = ps.rearrange("p (b d) -> p b d", b=4)
            copy("dve", vzl[:, 1:5, 0, 0:64], psv[:, :, 0:64], [pR], [vR[l]])
            copy("act", vzl[:, 1:5, 1, 64:128], psv[:, :, 64:128], [pR], [vR[l]])
            ai = 0
            for nb in range(4):
                first = (ti == 0 and nb == 0)
                nk = 128 if first else 256
                k0 = nb * 128 + (128 if first else 0)
                for c in range(4):
                    pso, pRo = psum()
                    for jh in range(2):
                        h = c + 4 * jh
                        a = ai % 2; ai += 1
                        S.op("pe", lambda g: g.matmul(pss[a][:, 0:nk], lhsT=qz[:, h, nb * 128:(nb + 1) * 128], rhs=kT[:, l, k0:k0 + nk],
                                                      start=True, stop=True), [qzR[h], kR[l]], [pssR[a]])
                        S.op("dve", lambda g: g.scalar_tensor_tensor(out=s_sb[:, a, 0:nk], in0=pss[a][:, 0:nk], scalar=ATT_SCALE,
                                                                     in1=biasm[:, h, 256 - nk:256], op0=ALU.mult, op1=ALU.add),
                             [pssR[a], smallR], [ssbR[a]])
                        sk = sinkb[:, l * 8 + h:l * 8 + h + 1]
                        S.op("dve", lambda g: g.reduce_max(out=stat[:, a, 0:1], in_=s_sb[:, a, 0:nk], axis=AX.X), [ssbR[a]], [statR[a]])
                        ts("dve", stat[:, a, 1:2], stat[:, a, 0:1], sk, -1.0, ALU.max, ALU.mult, [statR[a], smallR], [statR[a]])
                        act(p_f[:, a, 0:nk], s_sb[:, a, 0:nk], AF.Exp, [ssbR[a], statR[a]], [pfR[a], statR[a]], bias=stat[:, a, 1:2],
                            accum=stat[:, a, 2:3])
                        act(stat[:, a, 3:4], stat[:, a, 1:2], AF.Exp, [statR[a], smallR], [statR[a]], bias=sk)
                        tt("dve", stat[:, a, 4:5], stat[:, a, 2:3], stat[:, a, 3:4], ALU.add, [statR[a]], [statR[a]])
                        S.op("dve", lambda g: g.reciprocal(out=stat[:, a, 5:6], in_=stat[:, a, 4:5]), [statR[a]], [statR[a]])
                        ts("dve", p_b[:, a, 0:nk], p_f[:, a, 0:nk], stat[:, a, 5:6], None, ALU.mult, ALU.bypass, [pfR[a], statR[a]], [pbR[a]])
                        for kb in range(nk // 128):
                            S.op("pe", lambda g, kb=kb: g.transpose(pst[a][:, kb * 128:(kb + 1) * 128], p_b[:, a, kb * 128:(kb + 1) * 128], identb),
                                 [pbR[a], constR], [pstR[a]])
                        copy("act", pTs[:, a, 0:nk], pst[a][:, 0:nk], [pstR[a]], [pTsR[a]])
                        for kb in range(nk // 128):
                            blk = nb + kb + (1 if first else 0)
                            S.op("pe", lambda g, kb=kb, blk=blk: g.matmul(pso[:, 0:128], lhsT=vzl[:, blk, jh, :], rhs=pTs[:, a, kb * 128:(kb + 1) * 128],
                                                                          start=(jh == 0 and kb == 0), stop=(jh == 1 and kb == nk // 128 - 1)),
                                 [vR[l], pTsR[a]], [pRo])
                    copy(evac_eng(), ybr[2][:, c, nb * 128:(nb + 1) * 128], pso[:, 0:128], [pRo], [ybrR[2][c]])
            copy("pool", kT[:, l, 0:128], kT[:, l, NT:NT + 128], [kR[l]], [kR[l]])
            copy("pool", vz[:, l, 0:256], vz[:, l, 4 * 256:5 * 256], [vR[l]], [vR[l]])
            if d0:
                dump("yconv", ybr[1].rearrange("p c t -> p (c t)"), ybrR[1], [128, 4 * NT], BF16)
                dump("yattn", ybr[2].rearrange("p c t -> p (c t)"), ybrR[2], [128, 4 * NT], BF16)
                dump("yssm", ybr[0].rearrange("p c t -> p (c t)"), ybrR[0], [128, 4 * NT], BF16)
            for c in range(KC):
                wg, wgR = wload("w_in", l, 0, KC, 2816 + c * 384, 384)
                wb_, wbR = wload("w_br", l, 0, 4, c * 384, 384)
                j = c % 2
                for r in range(3):
                    col = r * 128
                    psg, pRg = psum()
                    group(psg, pRg, [wg[:, kc, col:col + 128] for kc in range(KC)], hrhs, [wgR] + hR)
                    psbr, pRb = psum()
                    group(psbr, pRb, [wb_[:, kc, col:col + 128] for kc in range(4)], [ybr[r][:, kc, :] for kc in range(4)], [wbR] + ybrR[r])
                    act(tmpA[:, j, :], psg, AF.Sigmoid, [pRg], [tmpAR[j]])
                    if r == 0:
                        tt("dve", acc[:, j, :], tmpA[:, j, :], psbr, ALU.mult, [tmpAR[j], pRb], [accR[j]])
                    else:
                        tt("dve", tmpB[:, j, :], tmpA[:, j, :], psbr, ALU.mult, [tmpAR[j], pRb], [tmpBR[j]])
                        if r == 1:
                            tt("pool", acc[:, j, :], acc[:, j, :], tmpB[:, j, :], ALU.add, [accR[j], tmpBR[j]], [accR[j]])
                        else:
                            tt("pool", mergedT[:, c, :], acc[:, j, :], tmpB[:, j, :], ALU.add, [accR[j], tmpBR[j]], [mgR[c]])
            mrhs = [mergedT[:, kc, :] for kc in range(KC)]
            for pc in range(2):
                wv, wR_ = wload("w_out", l, 0, KC, pc * 512, 512)
                for mo in range(4):
                    c = pc * 4 + mo
                    ps, pR = psum()
                    group(ps, pR, [wv[:, kc, mo * 128:(mo + 1) * 128] for kc in range(KC)], mrhs, [wR_] + mgR[0:KC])
                    tt("dve", xT[:, c, :], xT[:, c, :], ps, ALU.add, [xR[c], pR], [xR[c]])
            if d0:
                dump("x1", xT.rearrange("p c t -> p (c t)"), xR, [128, KC * NT])
            norm()
            norm_apply(1 * DEPTH + l)
            for half in range(2):
                for pc in range(HF // 2 + 1):
                    njj = 2 if pc < HF // 2 else 1
                    jf0 = half * HF + pc * 2
                    wv, wR_ = wload("w_fi", l, 0, KC, jf0 * 256, njj * 256)
                    for jj in range(njj):
                        jl = pc * 2 + jj
                        j = jl % 2
                        psa, pRa = psum(); psb_, pRb = psum()
                        group(psa, pRa, [wv[:, kc, jj * 256:jj * 256 + 128] for kc in range(KC)], hrhs, [wR_] + hR)
                        group(psb_, pRb, [wv[:, kc, jj * 256 + 128:jj * 256 + 256] for kc in range(KC)], hrhs, [wR_] + hR)
                        act(tmpA[:, j, :], psa, AF.Sigmoid, [pRa], [tmpAR[j]])
                        tt("dve", tmpA[:, j, :], tmpA[:, j, :], psa, ALU.mult, [tmpAR[j], pRa], [tmpAR[j]])
                        tt("dve", gT[:, jl, :], tmpA[:, j, :], psb_, ALU.mult, [tmpAR[j], pRb], [gR[jl]])
                grhs = [gT[:, kc, :] for kc in range(HF)]
                for pc in range(4):
                    wv, wR_ = wload("w_fo", l, half * HF * 128, HF, pc * 256, 256)
                    for mo in range(2):
                        c = pc * 2 + mo
                        ps, pR = psum()
                        group(ps, pR, [wv[:, kc, mo * 128:(mo + 1) * 128] for kc in range(HF)], grhs, [wR_] + gR)
                        tt("dve", xT[:, c, :], xT[:, c, :], ps, ALU.add, [xR[c], pR], [xR[c]])
            if d0:
                dump("x2", xT.rearrange("p c t -> p (c t)"), xR, [128, KC * NT])
            norm()
            norm_apply(2 * DEPTH + l)
            for kc in range(2):
                S.dma("pool", pTb[:, kc, :], pT_d[l, kc * 128:(kc + 1) * 128, t0:t0 + NT], W=[pTbR])
            wpp, wppR = wload("w_pp", l, 0, 2, 0, D)
            for pc in range(2):
                wv, wR_ = wload("w_pg", l, 0, KC, pc * 512, 512)
                for mo in range(4):
                    c = pc * 4 + mo
                    j = c % 2
                    ps, pR = psum()
                    group(ps, pR, [wv[:, kc, mo * 128:(mo + 1) * 128] for kc in range(KC)], hrhs, [wR_] + hR)
                    ps2, pR2 = psum()
                    group(ps2, pR2, [wpp[:, kc, c * 128:(c + 1) * 128] for kc in range(2)], [pTb[:, kc, :] for kc in range(2)], [wppR, pTbR])
                    act(tmpA[:, j, :], ps, AF.Sigmoid, [pR], [tmpAR[j]])
                    tt("dve", tmpB[:, j, :], tmpA[:, j, :], ps2, ALU.mult, [tmpAR[j], pR2], [tmpBR[j]])
                    tt("pool", xT[:, c, :], xT[:, c, :], tmpB[:, j, :], ALU.add, [xR[c], tmpBR[j]], [xR[c]])
        norm()
        for c in range(KC):
            j = c % 2
            S.op("dve", lambda g: g.scalar_tensor_tensor(out=tmpB[:, j, :], in0=xT[:, c, :],
                                                         scalar=gains[:, 3 * DEPTH * KC + c:3 * DEPTH * KC + c + 1], in1=rstd,
                                                         op0=ALU.mult, op1=ALU.mult), [xR[c], rstdR, smallR], [tmpBR[j]])
            S.dma("sp", outT_d[c * 128:(c + 1) * 128, t0:t0 + NT], tmpB[:, j, :], R=[tmpBR[j]], W=[], semres=tmpBR[j])
    for r in tmpBR + list(dbg.values()):
        if r.semcnt:
            nc.sync.wait_ge(r.sem, r.semcnt)
    return nc, S


def t5_bucket_np(dist):
    exact = 16
    df = np.maximum(dist, 1).astype(np.float32)
    large = exact + (np.log(df / exact) / math.log(128 / exact) * (32 - exact)).astype(np.int32)
    large = np.minimum(large, 31)
    return np.where(dist < exact, dist, large)


def prep_shared(inp, DEPTH):
    f = np.float32
    w_in = np.asarray(inp["w_in"], f)
    u = w_in[:, :, 0:512]
    cb = w_in[:, :, 512:1024]; cc = w_in[:, :, 1024:1536]; cx = w_in[:, :, 1536:2048]
    q = w_in[:, :, 2048:2560]; k = w_in[:, :, 2560:2688]; v = w_in[:, :, 2688:2816]
    gates = w_in[:, :, 2816:]
    conv = np.concatenate([np.concatenate([cb[:, :, c * 128:(c + 1) * 128], cc[:, :, c * 128:(c + 1) * 128],
                                           cx[:, :, c * 128:(c + 1) * 128]], -1) for c in range(4)], -1)
    hperm = [h for c in range(4) for h in (c, c + 4)]
    qp = np.concatenate([q[:, :, h * 64:(h + 1) * 64] for h in hperm], -1)
    gp = np.concatenate([gates[:, :, r * D + c * 128: r * D + (c + 1) * 128] for c in range(8) for r in range(3)], -1)
    w_in_p = np.ascontiguousarray(np.concatenate([u, conv, qp, k, v, gp], -1))
    wbr = np.asarray(inp["w_branch"], f).copy()
    wbr[:, 2] = np.concatenate([wbr[:, 2, h * 64:(h + 1) * 64, :] for h in hperm], 1)
    w_br_p = np.ascontiguousarray(np.concatenate([wbr[:, r, :, c * 128:(c + 1) * 128] for c in range(8) for r in range(3)], -1))
    wfi = np.asarray(inp["w_ffn_in"], f)
    w_fi_p = np.ascontiguousarray(np.concatenate([wfi[:, :, o + j * 128: o + (j + 1) * 128] for j in range(FC) for o in (0, FF)], -1))
    sh = {"w_in": w_in_p, "w_glu": np.ascontiguousarray(inp["ssm_w_glu"], f), "w_br": w_br_p,
          "w_out": np.ascontiguousarray(inp["w_out"], f), "w_fi": w_fi_p,
          "w_fo": np.ascontiguousarray(inp["w_ffn_out"], f), "w_pg": np.ascontiguousarray(inp["w_ple_gate"], f),
          "w_pp": np.ascontiguousarray(inp["w_ple_proj"], f)}
    norms = [np.asarray(inp[n], f) for n in ("norm_mix", "norm_ffn", "norm_ple")]
    gl = []
    for nm in norms:
        for l in range(DEPTH):
            gl.append(nm[l].reshape(KC, 128).T)
    gl.append(np.asarray(inp["norm_final"], f).reshape(KC, 128).T)
    sh["gains"] = np.ascontiguousarray(np.concatenate(gl, 1))
    cw = np.asarray(inp["conv_w"], f)
    sh["convw"] = np.ascontiguousarray(cw.reshape(DEPTH, 3, 4, 128).transpose(3, 0, 2, 1).reshape(128, DEPTH * 12))
    sh["dskip"] = np.ascontiguousarray(np.asarray(inp["ssm_d"], f).reshape(DEPTH, 4, 128).transpose(2, 0, 1).reshape(128, DEPTH * 4))
    sh["sinkb"] = np.ascontiguousarray(np.broadcast_to(np.asarray(inp["attn_sinks"], f).reshape(1, DEPTH * 8), (128, DEPTH * 8)))
    rb = np.asarray(inp["rel_bias"], f)
    qi = np.arange(128)[:, None]; kj = np.arange(256)[None, :]
    dist = qi + 128 - kj
    band = (dist >= 0) & (dist < 128)
    bucket = t5_bucket_np(np.clip(dist, 0, 127))
    bias = rb[bucket]
    bm = np.where(band[:, :, None], bias, f(-30000.0)).astype(f)
    sh["biasmask"] = np.ascontiguousarray(bm.transpose(0, 2, 1).reshape(128, 8 * 256))
    sh["identf"] = np.eye(128, dtype=f)
    lre = np.asarray(inp["ssm_lambda_re"], f); lim = np.asarray(inp["ssm_lambda_im"], f)
    ldt = np.asarray(inp["ssm_log_dt"], f)
    ldt_b = np.broadcast_to(ldt[:, :, None], (DEPTH, G, NS))

    def playout(a):
        return a.reshape(DEPTH, 16, 2, NS).transpose(2, 3, 0, 1).reshape(128, DEPTH * 16)
    sh["lamP"] = np.ascontiguousarray(np.concatenate([playout(lre), playout(lim), playout(ldt_b)], 1))
    lr = np.stack([lre.reshape(DEPTH, G * NS), lim.reshape(DEPTH, G * NS), ldt_b.reshape(DEPTH, G * NS)], 1)
    sh["lamR"] = np.ascontiguousarray(np.broadcast_to(lr[:, :, None, :], (DEPTH, 3, 128, G * NS)))
    bpad = np.zeros((DEPTH, 2, 128, G, NS), f)
    cpad = np.zeros((DEPTH, 2, 2, NS, 16, 128), f)
    bsrc = [np.asarray(inp["ssm_b_re"], f), np.asarray(inp["ssm_b_im"], f)]
    csrc = [np.asarray(inp["ssm_c_re"], f), np.asarray(inp["ssm_c_im"], f)]
    for g in range(G):
        g8 = g % 8
        for c in range(2):
            bpad[:, c, g8 * 16:(g8 + 1) * 16, g, :] = bsrc[c][:, g].transpose(0, 2, 1)
            cpad[:, c, g % 2, :, g // 2, g8 * 16:(g8 + 1) * 16] = csrc[c][:, g].transpose(0, 2, 1)
    sh["bpad"] = np.ascontiguousarray(bpad.reshape(DEPTH, 2, 128, G * NS))
    sh["cpad"] = np.ascontiguousarray(cpad.reshape(DEPTH, 2, 128, 16 * 128))
    return sh


_CACHE = {}


def run(inp, T, DEPTH, ncores, debug=None):
    sh = prep_shared(inp, DEPTH)
    x = np.asarray(inp["x"], np.float32); p = np.asarray(inp["p"], np.float32)
    in_maps = []
    for b in range(ncores):
        m = dict(sh)
        m["xT"] = np.ascontiguousarray(x[b].T)
        m["pT"] = np.ascontiguousarray(p[:, b].transpose(0, 2, 1))
        in_maps.append(m)
    key = (T, DEPTH, tuple(sorted(debug)) if debug else None)
    if key not in _CACHE:
        _CACHE[key] = build(T, DEPTH, debug)
    nc, S = _CACHE[key]
    res = run_bass_kernel_spmd(nc, in_maps, core_ids=list(range(ncores)))
    out = np.stack([np.ascontiguousarray(r["outT"].T) for r in res.results], 0)
    return out, res


def kernel(**inputs):
    x = inputs["x"]
    B, T, _ = x.shape
    DEPTH = inputs["w_in"].shape[0]
    out, _ = run(inputs, T, DEPTH, B)
    return out.astype(np.float32)
```

```python
import math
import numpy as np
import concourse.bass as bass
import concourse.mybir as mybir
from concourse.bass_utils import run_bass_kernel_spmd

F32 = mybir.dt.float32
BF16 = mybir.dt.bfloat16
AF = mybir.ActivationFunctionType
ALU = mybir.AluOpType
AX = mybir.AxisListType

D = 1024
KC = 8
NT = 512
BW = 512
FF = 2816
FC = 22
PLE = 256
G = 32
NS = 64
WIN = 5888
ATT_SCALE = 1.0 / 8.0
EPS = 1e-6
ERA = 30000
SLOT = 4096
ATTACH = {'pe', 'dve', 'act', 'pool'}
HF = 11


class Res:
    __slots__ = ("name", "w", "r", "sem", "semcnt", "excl")

    def __init__(self, name, sem=None, excl=False):
        self.name = name
        self.excl = excl
        self.w = None
        self.r = []
        self.sem = sem
        self.semcnt = 0


class Sched:
    def __init__(self, nc):
        self.nc = nc
        self.E = {"pe": nc.tensor, "dve": nc.vector, "act": nc.scalar, "pool": nc.gpsimd, "sp": nc.sync}
        self.nsem = 0
        self.dl = []
        self.sem = {k: self.newsem() for k in self.E}
        self.cnt = {k: 0 for k in self.E}
        self.seen = {k: {} for k in self.E}
        self.ninst = 0

    def newsem(self):
        self.nsem += 1
        return self.nc.alloc_semaphore("sm%d" % self.nsem)

    def dres(self, name):
        r = Res(name, self.newsem())
        self.dl.append(r)
        return r

    def barrier(self, skip=()):
        skip = set(id(x) for x in skip)
        for e, eng in self.E.items():
            seen = self.seen[e]
            for o in self.E:
                if o != e and self.cnt[o] > 0 and seen.get(self.sem[o].name, 0) < self.cnt[o]:
                    eng.wait_ge(self.sem[o], self.cnt[o])
                    seen[self.sem[o].name] = self.cnt[o]
            for r in self.dl:
                if id(r) in skip:
                    continue
                if r.semcnt > 0 and seen.get(r.sem.name, 0) < r.semcnt:
                    eng.wait_ge(r.sem, r.semcnt)
                    seen[r.sem.name] = r.semcnt

    def _need(self, e, R, W):
        need = {}
        seen = self.seen[e]

        def add(dep, raw):
            prod, sem, val = dep
            if prod == e and not raw:
                return
            if prod == "dma":
                val = sem[1].semcnt
                semh = sem[0]
            else:
                semh = sem
            key = semh.name
            if seen.get(key, 0) >= val:
                return
            if key not in need or need[key][1] < val:
                need[key] = (semh, val)
        for r in R:
            if r.w is not None:
                add(r.w, True)
        for w in W:
            if w.w is not None:
                add(w.w, False)
            for x in w.r:
                add(x, False)
        lst = list(need.values())
        for (semh, val) in lst:
            seen[semh.name] = val
        return lst

    def _emit(self, e, lst, ins_fn, attach=True):
        eng = self.E[e]
        attach = attach and (e in ATTACH)
        pre = lst[:-1] if attach else lst
        for (semh, val) in pre:
            eng.wait_ge(semh, val)
            self.ninst += 1
        ins = ins_fn(eng)
        if lst and attach:
            ins._wait_ge(lst[-1][0], lst[-1][1])
        self.ninst += 1
        return ins

    def op(self, e, fn, R=(), W=()):
        W = list(W) + [r for r in R if r.excl]
        R = [r for r in R if not r.excl]
        lst = self._need(e, R, W)
        if self.cnt[e] >= ERA:
            self.sem[e] = self.newsem()
            self.cnt[e] = 0
        ins = self._emit(e, lst, fn)
        self.cnt[e] += 1
        ins.then_inc(self.sem[e], 1)
        tag = (e, self.sem[e], self.cnt[e])
        for r in R:
            r.r.append(tag)
        for w in W:
            w.w = tag
            w.r = []
        return ins

    def dma(self, q, out, in_, R=(), W=(), semres=None):
        sr = semres if semres is not None else W[0]
        lst = self._need(q, R, W)
        ins = self._emit(q, lst, lambda g: g.dma_start(out=out, in_=in_), attach=False)
        sr.semcnt += 16
        ins.then_inc(sr.sem, 16)
        tag = ("dma", (sr.sem, sr), sr.semcnt)
        for r in R:
            r.r.append(tag)
        for w in W:
            w.w = tag
            w.r = []
        return ins


def build(T, DEPTH, debug=None):
    nc = bass.Bass("TRN2", target_bir_lowering=False)
    S = Sched(nc)
    ntiles = T // NT
    dbg = {}

    def din(name, shape, dt=F32):
        return nc.dram_tensor(name, list(shape), dt, kind="ExternalInput").ap()

    def dscr(name, shape, dt=BF16):
        return nc.dram_tensor(name, list(shape), dt, kind="Internal").ap()

    def sb(name, shape, dt=F32):
        return nc.alloc_sbuf_tensor(name, list(shape), dt).ap()

    xT_d = din("xT", [D, T])
    pT_d = din("pT", [DEPTH, PLE, T])
    outT_d = nc.dram_tensor("outT", [D, T], F32, kind="ExternalOutput").ap()
    wnames = {"w_in": (D, WIN), "w_glu": (BW, BW), "w_br": (BW, 3 * D), "w_out": (D, D),
              "w_fi": (D, 2 * FF), "w_fo": (FF, D), "w_pg": (D, D), "w_pp": (PLE, D)}
    w_f = {k: din(k, [DEPTH, v[0], v[1]]) for k, v in wnames.items()}
    w_b = {k: dscr(k + "_b", [DEPTH, v[0], v[1]]) for k, v in wnames.items()}
    w_res = {(k, l): S.dres("wr_%s%d" % (k, l)) for k in wnames for l in range(DEPTH)}
    gains_d = din("gains", [128, (3 * DEPTH + 1) * KC])
    convw_d = din("convw", [128, DEPTH * 4 * 3])
    dskip_d = din("dskip", [128, DEPTH * 4])
    sink_d = din("sinkb", [128, DEPTH * 8])
    bias_d = din("biasmask", [128, 8 * 256])
    identf_d = din("identf", [128, 128])
    lamP_d = din("lamP", [128, 3 * DEPTH * 16])
    lamR_d = din("lamR", [DEPTH, 3, 128, G * NS])
    bpad_d = din("bpad", [DEPTH, 2, 128, 16 * 128])
    cpad_d = din("cpad", [DEPTH, 2, 128, 16 * 128])
    sc_bb = dscr("sc_bb", [DEPTH, 128, 2 * 16 * 128])
    sc_cc = dscr("sc_cc", [DEPTH, 128, 2 * 16 * 128])
    sc_rot = dscr("sc_rot", [DEPTH, 128, 16 * 2 * NT], F32)

    xT = sb("xT_s", [128, KC, NT]); xR = [S.dres("x%d" % c) for c in range(KC)]
    hT = sb("hT_s", [128, KC, NT], BF16); hR = [Res("h%d" % c) for c in range(KC)]
    sq = sb("sq_s", [128, 2, NT], BF16); sqR = [Res("sq0"), Res("sq1")]
    rstd = sb("rstd_s", [128, NT]); rstdR = Res("rstd")
    ones_b = sb("ones_b", [128, 128], BF16); constR = Res("const")
    identf = sb("identf_s", [128, 128]); identb = sb("identb_s", [128, 128], BF16)
    gains = sb("gains_s", [128, (3 * DEPTH + 1) * KC])
    convw = sb("convw_s", [128, DEPTH * 12])
    dskip = sb("dskip_s", [128, DEPTH * 4])
    sinkb = sb("sink_s", [128, DEPTH * 8])
    biasm = sb("bias_s", [128, 8, 256])
    epsb = sb("eps_s", [128, 1])
    smallR = S.dres("small")
    uT = sb("uT_s", [128, 4, NT], BF16); uR = [Res("u%d" % c) for c in range(4)]
    ysb = sb("ysb_s", [128, NT]); ysbR = Res("ysb")
    ygT = sb("ygT_s", [128, 4, NT], BF16); ygR = [Res("yg%d" % c) for c in range(4)]
    qz = sb("qz_s", [128, 8, NT], BF16); qzR = [Res("qz%d" % h) for h in range(8)]
    kT = sb("kT_s", [128, DEPTH, NT + 128], BF16); kR = [Res("k%d" % l) for l in range(DEPTH)]
    vz = sb("vz_s", [128, DEPTH, 5 * 2 * 128], BF16); vR = [Res("v%d" % l) for l in range(DEPTH)]
    vcv = sb("vcv_s", [128, 2, NT + 2]); vcvR = [Res("vcv0"), Res("vcv1")]
    vhist = sb("vhist_s", [128, DEPTH * 4 * 2]); vhR = Res("vhist")
    acc = sb("acc_s", [128, 2, NT]); accR = [Res("acc0"), Res("acc1")]
    tmpA = sb("tmpA_s", [128, 2, NT]); tmpAR = [Res("tA0"), Res("tA1")]
    tmpB = sb("tmpB_s", [128, 2, NT]); tmpBR = [S.dres("tB0"), S.dres("tB1")]
    pTb = sb("pTb_s", [128, 2, NT], BF16); pTbR = S.dres("pTb")
    ARENA = 14336
    arena = sb("arena", [128, ARENA])
    NSLOT = 4
    wsl = [arena[:, i * 2048:(i + 1) * 2048].bitcast(BF16) for i in range(NSLOT)]
    wslR = [S.dres("wsl%d" % i) for i in range(NSLOT)]
    gT = arena[:, 8192:8192 + 2816].bitcast(BF16).rearrange("p (c t) -> p c t", t=NT)
    gR = [Res("g%d" % c) for c in range(HF)]
    mergedT = gT; mgR = gR
    ybr = [arena[:, 11008 + r * 1024:11008 + (r + 1) * 1024].bitcast(BF16).rearrange("p (c t) -> p c t", t=NT) for r in range(3)]
    ybrR = [[Res("ybr%d_%d" % (r, c)) for c in range(4)] for r in range(3)]
    bbS = sb("bbS", [128, 2, 16, 128], BF16); ccS = sb("ccS", [128, 2, 16, 128], BF16)
    bbR = S.dres("bbS"); ccR = S.dres("ccS")
    rot = sb("rot_s", [128, 2, 2, NT]); rotR = [S.dres("rot0"), S.dres("rot1")]
    rho = sb("rho_s", [128, DEPTH * 16])
    carry = sb("carry_s", [128, DEPTH * 16 * 2]); carryR = Res("carry")
    cst = sb("cst_s", [128, 4]); cstR = Res("cst")
    bre = sb("bre_s", [128, NT]); breR = Res("bre")
    bim = sb("bim_s", [128, NT]); bimR = Res("bim")
    t1 = sb("t1_s", [128, NT]); t1R = Res("t1")
    t2 = sb("t2_s", [128, NT]); t2R = Res("t2")
    t3 = sb("t3_s", [128, NT]); t3R = Res("t3")
    t4 = sb("t4_s", [128, NT]); t4R = Res("t4")
    gre = sb("gre_s", [128, NT]); greR = Res("gre")
    gim = sb("gim_s", [128, NT]); gimR = Res("gim")
    hre = sb("hre_s", [128, 2, NT], BF16); hreR = [Res("hre0"), Res("hre1")]
    nhi = sb("nhi_s", [128, 2, NT], BF16); nhiR = [Res("nhi0"), Res("nhi1")]
    s_sb = sb("ssb_s", [128, 2, 256]); ssbR = [Res("ssb0"), Res("ssb1")]
    p_f = sb("pf_s", [128, 2, 256]); pfR = [Res("pf0"), Res("pf1")]
    p_b = sb("pb_s", [128, 2, 256], BF16); pbR = [Res("pb0"), Res("pb1")]
    pTs = sb("pTs_s", [128, 2, 256], BF16); pTsR = [Res("pTs0"), Res("pTs1")]
    stat = sb("stat_s", [128, 2, 8]); statR = [Res("st0"), Res("st1")]
    NPS = 4
    psb = [nc.alloc_psum_tensor("ps%d" % i, [128, 512], F32).ap() for i in range(NPS)]
    psR = [Res("ps%d" % i, excl=True) for i in range(NPS)]
    psy = nc.alloc_psum_tensor("psy", [128, 512], F32).ap(); psyR = Res("psy", excl=True)
    pss = [nc.alloc_psum_tensor("pss%d" % i, [128, 512], F32).ap() for i in range(2)]
    pssR = [Res("pss0", excl=True), Res("pss1", excl=True)]
    pst_t = nc.alloc_psum_tensor("pst", [128, 1024], BF16).ap()
    pst = [pst_t[:, 0:256], pst_t[:, 0:256]]
    pstR = [Res("pst0", excl=True)] * 2
    st = {"ps": 0, "slot": 0, "ev": 0}

    def psum():
        i = st["ps"]; st["ps"] = (i + 1) % NPS
        return psb[i], psR[i]

    def evac_eng():
        st["ev"] ^= 1
        return "act" if st["ev"] else "dve"

    def copy(e, out, in_, R, W):
        if e == "act":
            return S.op("act", lambda g: g.activation(out=out, in_=in_, func=AF.Copy), R, W)
        return S.op(e, lambda g: g.tensor_copy(out=out, in_=in_), R, W)

    def dump(name, ap, R, shape, dt=F32):
        if debug is None or name not in debug:
            return
        d = nc.dram_tensor("dbg_" + name, list(shape), dt, kind="ExternalOutput").ap()
        r = S.dres("dbg_" + name)
        S.dma("sp", d, ap, R=R, W=[r])
        dbg[name] = r

    def tt(e, out, a, b, op, R, W):
        return S.op(e, lambda g: g.tensor_tensor(out=out, in0=a, in1=b, op=op), R, W)

    def ts(e, out, a, s1, s2, op0, op1, R, W):
        return S.op(e, lambda g: g.tensor_scalar(out=out, in0=a, scalar1=s1, scalar2=s2, op0=op0, op1=op1), R, W)

    def act(out, in_, func, R, W, bias=None, scale=None, accum=None):
        kw = {}
        if bias is not None:
            kw["bias"] = bias
        if scale is not None:
            kw["scale"] = scale
        if accum is not None:
            kw["accum_out"] = accum
        return S.op("act", lambda g: g.activation(out=out, in_=in_, func=func, **kw), R, W)

    for (dst, src) in [(gains, gains_d), (convw, convw_d), (dskip, dskip_d), (sinkb, sink_d),
                       (biasm.rearrange("p h j -> p (h j)"), bias_d), (identf, identf_d)]:
        S.dma("sp", dst, src, W=[smallR])
    S.op("dve", lambda g: g.memset(ones_b, 1.0), W=[constR])
    S.op("dve", lambda g: g.memset(epsb, EPS), W=[constR])
    S.op("dve", lambda g: g.tensor_copy(out=identb, in_=identf), R=[smallR], W=[constR])
    S.op("pool", lambda g: g.memset(qz.rearrange("p h t -> p (h t)"), 0.0), W=qzR)
    S.op("pool", lambda g: g.memset(kT.rearrange("p l t -> p (l t)"), 0.0), W=kR)
    S.op("pool", lambda g: g.memset(vz.rearrange("p l t -> p (l t)"), 0.0), W=vR)
    S.op("pool", lambda g: g.memset(vhist, 0.0), W=[vhR])
    S.op("pool", lambda g: g.memset(carry, 0.0), W=[carryR])

    for l in range(DEPTH):
        for k, (K, N) in wnames.items():
            rows = max(128, (1 << 20) // N // 128 * 128)
            r0 = 0
            while r0 < K:
                r1 = min(K, r0 + rows)
                S.dma("pool", w_b[k][l, r0:r1, :], w_f[k][l, r0:r1, :], W=[w_res[(k, l)]])
                r0 = r1
        r2 = S.dres("ccscr%d" % l)
        S.dma("pool", sc_cc[l].rearrange("p (c n) -> c p n", c=2), cpad_d[l], W=[r2])
        w_res[("cc", l)] = r2

    lamP = sb("lamP_s", [128, 3, DEPTH * 16]); lamPR = S.dres("lamP")
    S.dma("sp", lamP.rearrange("p a b -> p (a b)"), lamP_d, W=[lamPR])
    QN = 512
    LR = arena[:, 0:1536].rearrange("p (a n) -> p a n", a=3); LRR = S.dres("LR")
    wk = [arena[:, 1536 + i * 512:1536 + (i + 1) * 512] for i in range(8)]
    wkR = [Res("wk%d" % i) for i in range(8)]
    bpS = arena[:, 5632:6656].rearrange("p (a n) -> p a n", a=2); bpR = S.dres("bpS")
    bbo = arena[:, 6656:7168].bitcast(BF16).rearrange("p (a n) -> p a n", a=2); bboR = S.dres("bbo")
    rt = arena[:, 7168:11264].rearrange("p (a b c) -> p a b c", a=4, b=2); rtR = S.dres("rt")
    rtm = [arena[:, 11264 + i * 1024:11264 + (i + 1) * 1024].rearrange("p (a n) -> p a n", a=4) for i in range(3)]
    rtmR = [Res("rtm%d" % i) for i in range(3)]

    def cexp_unit(ang, c_out, s_out, tmp, RA, Rc, Rs, Rt, n_sq=3):
        sc = 1.0 / (1 << n_sq)
        act(s_out, ang, AF.Sin, [RA], [Rs], scale=sc)
        act(tmp, ang, AF.Sin, [RA], [Rt], scale=sc * 0.5)
        tt("dve", tmp, tmp, tmp, ALU.mult, [Rt], [Rt])
        ts("dve", c_out, tmp, -2.0, 1.0, ALU.mult, ALU.add, [Rt], [Rc])
        for _ in range(n_sq):
            tt("dve", tmp, c_out, s_out, ALU.mult, [Rc, Rs], [Rt])
            tt("dve", c_out, c_out, c_out, ALU.mult, [Rc], [Rc])
            tt("dve", s_out, s_out, s_out, ALU.mult, [Rs], [Rs])
            tt("dve", c_out, c_out, s_out, ALU.subtract, [Rc, Rs], [Rc])
            ts("dve", s_out, tmp, 2.0, None, ALU.mult, ALU.bypass, [Rt], [Rs])

    NL = DEPTH * 16
    pw = [sb("pw%d" % i, [128, NL]) for i in range(6)]
    pwR = [Res("pw%d" % i) for i in range(6)]
    rhoR = Res("rho")
    act(pw[0], lamP[:, 2, :], AF.Exp, [lamPR], [pwR[0]])
    tt("dve", pw[1], lamP[:, 0, :], pw[0], ALU.mult, [lamPR, pwR[0]], [pwR[1]])
    act(rho, pw[1], AF.Exp, [pwR[1]], [rhoR])
    tt("dve", pw[2], lamP[:, 1, :], pw[0], ALU.mult, [lamPR, pwR[0]], [pwR[2]])
    cexp_unit(pw[2], pw[3], pw[4], pw[5], pwR[2], pwR[3], pwR[4], pwR[5])
    for l in range(DEPTH):
        rotres_l = S.dres("rotscr%d" % l)
        w_res[("rot", l)] = rotres_l
        for gq in range(4):
            cs = pw[3][:, l * 16 + gq * 4:l * 16 + gq * 4 + 4]
            sn = pw[4][:, l * 16 + gq * 4:l * 16 + gq * 4 + 4]
            S.op("dve", lambda g: g.tensor_copy(out=rt[:, :, 0, 0], in_=cs), [pwR[3]], [rtR])
            ts("dve", rt[:, :, 1, 0], sn, -1.0, None, ALU.mult, ALU.bypass, [pwR[4]], [rtR])
            n = 1
            while n < NT:
                fc = rt[:, :, 0, 0:n]; fs = rt[:, :, 1, 0:n]
                mc = rt[:, :, 0, n - 1:n].to_broadcast([128, 4, n]); ms = rt[:, :, 1, n - 1:n].to_broadcast([128, 4, n])
                a0 = rtm[0][:, :, 0:n]; a1 = rtm[1][:, :, 0:n]; a2 = rtm[2][:, :, 0:n]
                tt("dve", a0, fc, mc, ALU.mult, [rtR], [rtmR[0]])
                tt("dve", a1, fs, ms, ALU.mult, [rtR], [rtmR[1]])
                tt("dve", a2, fc, ms, ALU.mult, [rtR], [rtmR[2]])
                tt("dve", rt[:, :, 0, n:2 * n], a0, a1, ALU.subtract, [rtmR[0], rtmR[1]], [rtR])
                tt("dve", a0, fs, mc, ALU.mult, [rtR], [rtmR[0]])
                tt("dve", rt[:, :, 1, n:2 * n], a2, a0, ALU.add, [rtmR[2], rtmR[0]], [rtR])
                n *= 2
            S.dma("sp", sc_rot[l, :, gq * 4 * 2 * NT:(gq + 1) * 4 * 2 * NT], rt.rearrange("p a b c -> p (a b c)"),
                  R=[rtR], W=[rotres_l])
        r1 = S.dres("bbscr%d" % l)
        w_res[("bb", l)] = r1
        for qd in range(4):
            for a in range(3):
                S.dma("sp", LR[:, a, :], lamR_d[l, a, :, qd * QN:(qd + 1) * QN], W=[LRR])
            for c in range(2):
                S.dma("sp", bpS[:, c, :], bpad_d[l, c, :, qd * QN:(qd + 1) * QN], W=[bpR])
            lre = LR[:, 0, :]; lim = LR[:, 1, :]
            act(wk[0], LR[:, 2, :], AF.Exp, [LRR], [wkR[0]])
            tt("dve", wk[1], lre, wk[0], ALU.mult, [LRR, wkR[0]], [wkR[1]])
            act(wk[1], wk[1], AF.Exp, [wkR[1]], [wkR[1]])
            tt("dve", wk[2], lim, wk[0], ALU.mult, [LRR, wkR[0]], [wkR[2]])
            cexp_unit(wk[2], wk[3], wk[4], wk[5], wkR[2], wkR[3], wkR[4], wkR[5])
            tt("dve", wk[3], wk[3], wk[1], ALU.mult, [wkR[3], wkR[1]], [wkR[3]])
            tt("dve", wk[4], wk[4], wk[1], ALU.mult, [wkR[4], wkR[1]], [wkR[4]])
            ts("dve", wk[3], wk[3], -1.0, None, ALU.add, ALU.bypass, [wkR[3]], [wkR[3]])
            tt("dve", wk[0], lre, lre, ALU.mult, [LRR], [wkR[0]])
            tt("dve", wk[1], lim, lim, ALU.mult, [LRR], [wkR[1]])
            tt("dve", wk[0], wk[0], wk[1], ALU.add, [wkR[0], wkR[1]], [wkR[0]])
            S.op("dve", lambda g: g.reciprocal(out=wk[0], in_=wk[0]), [wkR[0]], [wkR[0]])
            tt("dve", wk[1], wk[3], lre, ALU.mult, [wkR[3], LRR], [wkR[1]])
            tt("dve", wk[2], wk[4], lim, ALU.mult, [wkR[4], LRR], [wkR[2]])
            tt("dve", wk[1], wk[1], wk[2], ALU.add, [wkR[1], wkR[2]], [wkR[1]])
            tt("dve", wk[1], wk[1], wk[0], ALU.mult, [wkR[1], wkR[0]], [wkR[1]])
            tt("dve", wk[2], wk[4], lre, ALU.mult, [wkR[4], LRR], [wkR[2]])
            tt("dve", wk[5], wk[3], lim, ALU.mult, [wkR[3], LRR], [wkR[5]])
            tt("dve", wk[2], wk[2], wk[5], ALU.subtract, [wkR[2], wkR[5]], [wkR[2]])
            tt("dve", wk[2], wk[2], wk[0], ALU.mult, [wkR[2], wkR[0]], [wkR[2]])
            tt("dve", wk[3], wk[1], bpS[:, 0, :], ALU.mult, [wkR[1], bpR], [wkR[3]])
            tt("dve", wk[4], wk[2], bpS[:, 1, :], ALU.mult, [wkR[2], bpR], [wkR[4]])
            tt("dve", bbo[:, 0, :], wk[3], wk[4], ALU.subtract, [wkR[3], wkR[4]], [bboR])
            tt("dve", wk[3], wk[1], bpS[:, 1, :], ALU.mult, [wkR[1], bpR], [wkR[3]])
            tt("dve", wk[4], wk[2], bpS[:, 0, :], ALU.mult, [wkR[2], bpR], [wkR[4]])
            tt("dve", bbo[:, 1, :], wk[3], wk[4], ALU.add, [wkR[3], wkR[4]], [bboR])
            dst = sc_bb[l].rearrange("p (c n) -> p c n", c=2)[:, :, qd * QN:(qd + 1) * QN]
            S.dma("sp", dst, bbo, R=[bboR], W=[r1])
    S.barrier(skip=[w_res[(k, l)] for k in wnames for l in range(DEPTH)])

    def wload(k, l, r0, nkc, c0, ncols):
        i = st["slot"]; st["slot"] = (i + 1) % NSLOT
        assert nkc * ncols <= SLOT
        src = w_b[k][l, r0:r0 + nkc * 128, c0:c0 + ncols].rearrange("(kc p) n -> p kc n", p=128)
        dst = wsl[i][:, 0:nkc * ncols].rearrange("p (kc n) -> p kc n", n=ncols)
        S.dma("sp", dst, src, R=[w_res[(k, l)]], W=[wslR[i]])
        return dst, wslR[i]

    def group(out_ap, outR, lhs_list, rhs_list, R):
        n = len(lhs_list)
        for i in range(n):
            S.op("pe", lambda g, i=i: g.matmul(out_ap, lhsT=lhs_list[i], rhs=rhs_list[i], start=(i == 0), stop=(i == n - 1)),
                 R, [outR])

    def norm():
        ps, pR = psum()
        for c in range(KC):
            j = c % 2
            act(sq[:, j, :], xT[:, c, :], AF.Square, [xR[c]], [sqR[j]])
            S.op("pe", lambda g, c=c, j=j: g.matmul(ps, lhsT=ones_b, rhs=sq[:, j, :], start=(c == 0), stop=(c == KC - 1)),
                 [constR, sqR[j]], [pR])
        act(rstd, ps, AF.Sqrt, [pR, constR], [rstdR], bias=epsb[:, 0:1], scale=1.0 / D)
        S.op("dve", lambda g: g.reciprocal(out=rstd, in_=rstd), [rstdR], [rstdR])

    def norm_apply(gidx):
        for c in range(KC):
            S.op("dve", lambda g, c=c: g.scalar_tensor_tensor(
                out=hT[:, c, :], in0=xT[:, c, :], scalar=gains[:, gidx * KC + c:gidx * KC + c + 1], in1=rstd,
                op0=ALU.mult, op1=ALU.mult), [xR[c], rstdR, smallR], [hR[c]])

    hrhs = [hT[:, c, :] for c in range(KC)]

    for ti in range(ntiles):
        t0 = ti * NT
        for c in range(KC):
            S.dma("sp", xT[:, c, :], xT_d[c * 128:(c + 1) * 128, t0:t0 + NT], W=[xR[c]])
        for l in range(DEPTH):
            d0 = (ti == 0 and l == 0)
            norm()
            norm_apply(0 * DEPTH + l)
            if d0:
                dump("h0", hT.rearrange("p c t -> p (c t)"), hR, [128, KC * NT], BF16)
            S.dma("pool", bbS.rearrange("p a b c -> p (a b c)"), sc_bb[l], R=[w_res[("bb", l)]], W=[bbR])
            S.dma("pool", ccS.rearrange("p a b c -> p (a b c)"), sc_cc[l], R=[w_res[("cc", l)]], W=[ccR])
            wv, wR_ = wload("w_in", l, 0, KC, 0, 512)
            for mo in range(4):
                ps, pR = psum()
                group(ps, pR, [wv[:, kc, mo * 128:(mo + 1) * 128] for kc in range(KC)], hrhs, [wR_] + hR)
                copy(evac_eng(), uT[:, mo, :], ps, [pR], [uR[mo]])
            for gp in range(16):
                ch = gp // 4
                j = gp % 2
                S.dma("pool", rot[:, j].rearrange("p a t -> p (a t)"), sc_rot[l, :, gp * 2 * NT:(gp + 1) * 2 * NT],
                      R=[w_res[("rot", l)]], W=[rotR[j]])
                Fc = rot[:, j, 0, :]; Fs = rot[:, j, 1, :]
                ps_r, pRr = psum()
                S.op("pe", lambda g: g.matmul(ps_r, lhsT=bbS[:, 0, gp, :], rhs=uT[:, ch, :], start=True, stop=True),
                     [bbR, uR[ch]], [pRr])
                ps_i, pRi = psum()
                S.op("pe", lambda g: g.matmul(ps_i, lhsT=bbS[:, 1, gp, :], rhs=uT[:, ch, :], start=True, stop=True),
                     [bbR, uR[ch]], [pRi])
                copy("act", bre, ps_r, [pRr], [breR])
                copy("act", bim, ps_i, [pRi], [bimR])
                tt("dve", t1, bre, Fc, ALU.mult, [breR, rotR[j]], [t1R])
                tt("pool", t2, bim, Fs, ALU.mult, [bimR, rotR[j]], [t2R])
                tt("dve", gre, t1, t2, ALU.subtract, [t1R, t2R], [greR])
                tt("pool", t3, bre, Fs, ALU.mult, [breR, rotR[j]], [t3R])
                tt("dve", t4, bim, Fc, ALU.mult, [bimR, rotR[j]], [t4R])
                tt("dve", gim, t3, t4, ALU.add, [t3R, t4R], [gimR])
                rb = rho[:, l * 16 + gp:l * 16 + gp + 1].to_broadcast([128, NT])
                ci = (l * 16 + gp) * 2
                S.op("dve", lambda g: g.tensor_tensor_scan(out=t1, data0=rb, data1=gre, initial=carry[:, ci:ci + 1],
                                                           op0=ALU.mult, op1=ALU.add), [greR, rhoR, carryR], [t1R])
                S.op("dve", lambda g: g.tensor_tensor_scan(out=t2, data0=rb, data1=gim, initial=carry[:, ci + 1:ci + 2],
                                                           op0=ALU.mult, op1=ALU.add), [gimR, rhoR, carryR], [t2R])
                L1 = slice(NT - 1, NT)
                tt("pool", cst[:, 0:1], t1[:, L1], Fc[:, L1], ALU.mult, [t1R, rotR[j]], [cstR])
                tt("pool", cst[:, 1:2], t2[:, L1], Fs[:, L1], ALU.mult, [t2R, rotR[j]], [cstR])
                tt("pool", cst[:, 2:3], t2[:, L1], Fc[:, L1], ALU.mult, [t2R, rotR[j]], [cstR])
                tt("pool", cst[:, 3:4], t1[:, L1], Fs[:, L1], ALU.mult, [t1R, rotR[j]], [cstR])
                tt("pool", carry[:, ci:ci + 1], cst[:, 0:1], cst[:, 1:2], ALU.add, [cstR], [carryR])
                tt("pool", carry[:, ci + 1:ci + 2], cst[:, 2:3], cst[:, 3:4], ALU.subtract, [cstR], [carryR])
                tt("dve", t3, t1, Fc, ALU.mult, [t1R, rotR[j]], [t3R])
                tt("pool", t4, t2, Fs, ALU.mult, [t2R, rotR[j]], [t4R])
                tt("dve", hre[:, j, :], t3, t4, ALU.add, [t3R, t4R], [hreR[j]])
                tt("pool", gre, t1, Fs, ALU.mult, [t1R, rotR[j]], [greR])
                tt("dve", gim, t2, Fc, ALU.mult, [t2R, rotR[j]], [gimR])
                tt("dve", nhi[:, j, :], gre, gim, ALU.subtract, [greR, gimR], [nhiR[j]])
                S.op("pe", lambda g: g.matmul(psy, lhsT=ccS[:, 0, gp, :], rhs=hre[:, j, :], start=(gp % 4 == 0), stop=False),
                     [ccR, hreR[j]], [psyR])
                S.op("pe", lambda g: g.matmul(psy, lhsT=ccS[:, 1, gp, :], rhs=nhi[:, j, :], start=False, stop=(gp % 4 == 3)),
                     [ccR, nhiR[j]], [psyR])
                if gp % 4 == 3:
                    S.op("dve", lambda g: g.scalar_tensor_tensor(out=ysb, in0=uT[:, ch, :], scalar=dskip[:, l * 4 + ch:l * 4 + ch + 1],
                                                                 in1=psy, op0=ALU.mult, op1=ALU.add), [uR[ch], psyR, smallR], [ysbR])
                    if d0 and ch == 0:
                        dump("yssm0", ysb, [ysbR], [128, NT])
                    act(t3, ysb, AF.Square, [ysbR], [t3R])
                    ts("dve", t3, t3, 0.044715, 1.0, ALU.mult, ALU.add, [t3R], [t3R])
                    tt("pool", t3, t3, ysb, ALU.mult, [t3R, ysbR], [t3R])
                    act(t4, t3, AF.Sigmoid, [t3R], [t4R], scale=1.5957691216057308)
                    tt("dve", ygT[:, ch, :], ysb, t4, ALU.mult, [ysbR, t4R], [ygR[ch]])
            wv, wR_ = wload("w_glu", l, 0, 4, 0, 512)
            for mo in range(4):
                ps, pR = psum()
                group(ps, pR, [wv[:, kc, mo * 128:(mo + 1) * 128] for kc in range(4)], [ygT[:, kc, :] for kc in range(4)], [wR_] + ygR)
                j = mo % 2
                act(tmpA[:, j, :], ps, AF.Sigmoid, [pR], [tmpAR[j]])
                tt("dve", ybr[0][:, mo, :], ygT[:, mo, :], tmpA[:, j, :], ALU.mult, [ygR[mo], tmpAR[j]], [ybrR[0][mo]])
            for c in range(4):
                wv, wR_ = wload("w_in", l, 0, KC, 512 + c * 384, 384)
                j = c % 2
                psB, pRB = psum(); psC, pRC = psum(); psX, pRX = psum()
                group(psC, pRC, [wv[:, kc, 128:256] for kc in range(KC)], hrhs, [wR_] + hR)
                group(psX, pRX, [wv[:, kc, 256:384] for kc in range(KC)], hrhs, [wR_] + hR)
                group(psB, pRB, [wv[:, kc, 0:128] for kc in range(KC)], hrhs, [wR_] + hR)
                copy("act", tmpA[:, j, :], psC, [pRC], [tmpAR[j]])
                hi = (l * 4 + c) * 2
                copy("pool", vcv[:, j, 0:2], vhist[:, hi:hi + 2], [vhR], [vcvR[j]])
                tt("dve", vcv[:, j, 2:NT + 2], tmpA[:, j, :], psX, ALU.mult, [tmpAR[j], pRX], [vcvR[j]])
                copy("pool", vhist[:, hi:hi + 2], vcv[:, j, NT:NT + 2], [vcvR[j]], [vhR])
                wi = (l * 4 + c) * 3
                act(acc[:, j, :], vcv[:, j, 0:NT], AF.Copy, [vcvR[j], smallR], [accR[j]], scale=convw[:, wi:wi + 1])
                S.op("dve", lambda g: g.scalar_tensor_tensor(out=acc[:, j, :], in0=vcv[:, j, 1:NT + 1], scalar=convw[:, wi + 1:wi + 2],
                                                             in1=acc[:, j, :], op0=ALU.mult, op1=ALU.add), [vcvR[j], accR[j], smallR], [accR[j]])
                S.op("dve", lambda g: g.scalar_tensor_tensor(out=acc[:, j, :], in0=vcv[:, j, 2:NT + 2], scalar=convw[:, wi + 2:wi + 3],
                                                             in1=acc[:, j, :], op0=ALU.mult, op1=ALU.add), [vcvR[j], accR[j], smallR], [accR[j]])
                tt("dve", ybr[1][:, c, :], acc[:, j, :], psB, ALU.mult, [accR[j], pRB], [ybrR[1][c]])
            wv, wR_ = wload("w_in", l, 0, KC, 2048, 512)
            for c in range(4):
                ps, pR = psum()
                group(ps, pR, [wv[:, kc, c * 128:(c + 1) * 128] for kc in range(KC)], hrhs, [wR_] + hR)
                copy("act", qz[0:64, c, :], ps[0:64, :], [pR], [qzR[c]])
                copy("dve", qz[64:128, c + 4, :], ps[64:128, :], [pR], [qzR[c + 4]])
            wv, wR_ = wload("w_in", l, 0, KC, 2560, 256)
            ps, pR = psum()
            group(ps, pR, [wv[:, kc, 0:128] for kc in range(KC)], hrhs, [wR_] + hR)
            copy("act", kT[:, l, 128:128 + NT], ps, [pR], [kR[l]])
            ps, pR = psum()
            for blk in range(4):
                group(ps[:, blk * 128:(blk + 1) * 128], pR, [hT[:, kc, blk * 128:(blk + 1) * 128] for kc in range(KC)],
                      [wv[:, kc, 128:256] for kc in range(KC)], [wR_] + hR)
            vzl = vz[:, l, :].rearrange("p (b k d) -> p b k d", b=5, k=2)
            psv = ps.rearrange("p (b d) -> p b d", b=4)
            copy("dve", vzl[:, 1:5, 0, 0:64], psv[:, :, 0:64], [pR], [vR[l]])
            copy("act", vzl[:, 1:5, 1, 64:128], psv[:, :, 64:128], [pR], [vR[l]])
            ai = 0
            for nb in range(4):
                first = (ti == 0 and nb == 0)
                nk = 128 if first else 256
                k0 = nb * 128 + (128 if first else 0)
                for c in range(4):
                    pso, pRo = psum()
                    for jh in range(2):
                        h = c + 4 * jh
                        a = ai % 2; ai += 1
                        S.op("pe", lambda g: g.matmul(pss[a][:, 0:nk], lhsT=qz[:, h, nb * 128:(nb + 1) * 128], rhs=kT[:, l, k0:k0 + nk],
                                                      start=True, stop=True), [qzR[h], kR[l]], [pssR[a]])
                        S.op("dve", lambda g: g.scalar_tensor_tensor(out=s_sb[:, a, 0:nk], in0=pss[a][:, 0:nk], scalar=ATT_SCALE,
                                                                     in1=biasm[:, h, 256 - nk:256], op0=ALU.mult, op1=ALU.add),
                             [pssR[a], smallR], [ssbR[a]])
                        sk = sinkb[:, l * 8 + h:l * 8 + h + 1]
                        S.op("dve", lambda g: g.reduce_max(out=stat[:, a, 0:1], in_=s_sb[:, a, 0:nk], axis=AX.X), [ssbR[a]], [statR[a]])
                        ts("dve", stat[:, a, 1:2], stat[:, a, 0:1], sk, -1.0, ALU.max, ALU.mult, [statR[a], smallR], [statR[a]])
                        act(p_f[:, a, 0:nk], s_sb[:, a, 0:nk], AF.Exp, [ssbR[a], statR[a]], [pfR[a], statR[a]], bias=stat[:, a, 1:2],
                            accum=stat[:, a, 2:3])
                        act(stat[:, a, 3:4], stat[:, a, 1:2], AF.Exp, [statR[a], smallR], [statR[a]], bias=sk)
                        tt("dve", stat[:, a, 4:5], stat[:, a, 2:3], stat[:, a, 3:4], ALU.add, [statR[a]], [statR[a]])
                        S.op("dve", lambda g: g.reciprocal(out=stat[:, a, 5:6], in_=stat[:, a, 4:5]), [statR[a]], [statR[a]])
                        ts("dve", p_b[:, a, 0:nk], p_f[:, a, 0:nk], stat[:, a, 5:6], None, ALU.mult, ALU.bypass, [pfR[a], statR[a]], [pbR[a]])
                        for kb in range(nk // 128):
                            S.op("pe", lambda g, kb=kb: g.transpose(pst[a][:, kb * 128:(kb + 1) * 128], p_b[:, a, kb * 128:(kb + 1) * 128], identb),
                                 [pbR[a], constR], [pstR[a]])
                        copy("act", pTs[:, a, 0:nk], pst[a][:, 0:nk], [pstR[a]], [pTsR[a]])
                        for kb in range(nk // 128):
                            blk = nb + kb + (1 if first else 0)
                            S.op("pe", lambda g, kb=kb, blk=blk: g.matmul(pso[:, 0:128], lhsT=vzl[:, blk, jh, :], rhs=pTs[:, a, kb * 128:(kb + 1) * 128],
                                                                          start=(jh == 0 and kb == 0), stop=(jh == 1 and kb == nk // 128 - 1)),
                                 [vR[l], pTsR[a]], [pRo])
                    copy(evac_eng(), ybr[2][:, c, nb * 128:(nb + 1) * 128], pso[:, 0:128], [pRo], [ybrR[2][c]])
            copy("pool", kT[:, l, 0:128], kT[:, l, NT:NT + 128], [kR[l]], [kR[l]])
            copy("pool", vz[:, l, 0:256], vz[:, l, 4 * 256:5 * 256], [vR[l]], [vR[l]])
            if d0:
                dump("yconv", ybr[1].rearrange("p c t -> p (c t)"), ybrR[1], [128, 4 * NT], BF16)
                dump("yattn", ybr[2].rearrange("p c t -> p (c t)"), ybrR[2], [128, 4 * NT], BF16)
                dump("yssm", ybr[0].rearrange("p c t -> p (c t)"), ybrR[0], [128, 4 * NT], BF16)
            for c in range(KC):
                wg, wgR = wload("w_in", l, 0, KC, 2816 + c * 384, 384)
                wb_, wbR = wload("w_br", l, 0, 4, c * 384, 384)
                j = c % 2
                for r in range(3):
                    col = r * 128
                    psg, pRg = psum()
                    group(psg, pRg, [wg[:, kc, col:col + 128] for kc in range(KC)], hrhs, [wgR] + hR)
                    psbr, pRb = psum()
                    group(psbr, pRb, [wb_[:, kc, col:col + 128] for kc in range(4)], [ybr[r][:, kc, :] for kc in range(4)], [wbR] + ybrR[r])
                    act(tmpA[:, j, :], psg, AF.Sigmoid, [pRg], [tmpAR[j]])
                    if r == 0:
                        tt("dve", acc[:, j, :], tmpA[:, j, :], psbr, ALU.mult, [tmpAR[j], pRb], [accR[j]])
                    else:
                        tt("dve", tmpB[:, j, :], tmpA[:, j, :], psbr, ALU.mult, [tmpAR[j], pRb], [tmpBR[j]])
                        if r == 1:
                            tt("pool", acc[:, j, :], acc[:, j, :], tmpB[:, j, :], ALU.add, [accR[j], tmpBR[j]], [accR[j]])
                        else:
                            tt("pool", mergedT[:, c, :], acc[:, j, :], tmpB[:, j, :], ALU.add, [accR[j], tmpBR[j]], [mgR[c]])
            mrhs = [mergedT[:, kc, :] for kc in range(KC)]
            for pc in range(2):
                wv, wR_ = wload("w_out", l, 0, KC, pc * 512, 512)
                for mo in range(4):
                    c = pc * 4 + mo
                    ps, pR = psum()
                    group(ps, pR, [wv[:, kc, mo * 128:(mo + 1) * 128] for kc in range(KC)], mrhs, [wR_] + mgR[0:KC])
                    tt("dve", xT[:, c, :], xT[:, c, :], ps, ALU.add, [xR[c], pR], [xR[c]])
            if d0:
                dump("x1", xT.rearrange("p c t -> p (c t)"), xR, [128, KC * NT])
            norm()
            norm_apply(1 * DEPTH + l)
            for half in range(2):
                for pc in range(HF // 2 + 1):
                    njj = 2 if pc < HF // 2 else 1
                    jf0 = half * HF + pc * 2
                    wv, wR_ = wload("w_fi", l, 0, KC, jf0 * 256, njj * 256)
                    for jj in range(njj):
                        jl = pc * 2 + jj
                        j = jl % 2
                        psa, pRa = psum(); psb_, pRb = psum()
                        group(psa, pRa, [wv[:, kc, jj * 256:jj * 256 + 128] for kc in range(KC)], hrhs, [wR_] + hR)
                        group(psb_, pRb, [wv[:, kc, jj * 256 + 128:jj * 256 + 256] for kc in range(KC)], hrhs, [wR_] + hR)
                        act(tmpA[:, j, :], psa, AF.Sigmoid, [pRa], [tmpAR[j]])
                        tt("dve", tmpA[:, j, :], tmpA[:, j, :], psa, ALU.mult, [tmpAR[j], pRa], [tmpAR[j]])
                        tt("dve", gT[:, jl, :], tmpA[:, j, :], psb_, ALU.mult, [tmpAR[j], pRb], [gR[jl]])
                grhs = [gT[:, kc, :] for kc in range(HF)]
                for pc in range(4):
                    wv, wR_ = wload("w_fo", l, half * HF * 128, HF, pc * 256, 256)
                    for mo in range(2):
                        c = pc * 2 + mo
                        ps, pR = psum()
                        group(ps, pR, [wv[:, kc, mo * 128:(mo + 1) * 128] for kc in range(HF)], grhs, [wR_] + gR)
                        tt("dve", xT[:, c, :], xT[:, c, :], ps, ALU.add, [xR[c], pR], [xR[c]])
            if d0:
                dump("x2", xT.rearrange("p c t -> p (c t)"), xR, [128, KC * NT])
            norm()
            norm_apply(2 * DEPTH + l)
            for kc in range(2):
                S.dma("pool", pTb[:, kc, :], pT_d[l, kc * 128:(kc + 1) * 128, t0:t0 + NT], W=[pTbR])
            wpp, wppR = wload("w_pp", l, 0, 2, 0, D)
            for pc in range(2):
                wv, wR_ = wload("w_pg", l, 0, KC, pc * 512, 512)
                for mo in range(4):
                    c = pc * 4 + mo
                    j = c % 2
                    ps, pR = psum()
                    group(ps, pR, [wv[:, kc, mo * 128:(mo + 1) * 128] for kc in range(KC)], hrhs, [wR_] + hR)
                    ps2, pR2 = psum()
                    group(ps2, pR2, [wpp[:, kc, c * 128:(c + 1) * 128] for kc in range(2)], [pTb[:, kc, :] for kc in range(2)], [wppR, pTbR])
                    act(tmpA[:, j, :], ps, AF.Sigmoid, [pR], [tmpAR[j]])
                    tt("dve", tmpB[:, j, :], tmpA[:, j, :], ps2, ALU.mult, [tmpAR[j], pR2], [tmpBR[j]])
                    tt("pool", xT[:, c, :], xT[:, c, :], tmpB[:, j, :], ALU.add, [xR[c], tmpBR[j]], [xR[c]])
        norm()
        for c in range(KC):
            j = c % 2
            S.op("dve", lambda g: g.scalar_tensor_tensor(out=tmpB[:, j, :], in0=xT[:, c, :],
                                                         scalar=gains[:, 3 * DEPTH * KC + c:3 * DEPTH * KC + c + 1], in1=rstd,
                                                         op0=ALU.mult, op1=ALU.mult), [xR[c], rstdR, smallR], [tmpBR[j]])
            S.dma("sp", outT_d[c * 128:(c + 1) * 128, t0:t0 + NT], tmpB[:, j, :], R=[tmpBR[j]], W=[], semres=tmpBR[j])
    for r in tmpBR + list(dbg.values()):
        if r.semcnt:
            nc.sync.wait_ge(r.sem, r.semcnt)
    return nc, S


def t5_bucket_np(dist):
    exact = 16
    df = np.maximum(dist, 1).astype(np.float32)
    large = exact + (np.log(df / exact) / math.log(128 / exact) * (32 - exact)).astype(np.int32)
    large = np.minimum(large, 31)
    return np.where(dist < exact, dist, large)


def prep_shared(inp, DEPTH):
    f = np.float32
    w_in = np.asarray(inp["w_in"], f)
    u = w_in[:, :, 0:512]
    cb = w_in[:, :, 512:1024]; cc = w_in[:, :, 1024:1536]; cx = w_in[:, :, 1536:2048]
    q = w_in[:, :, 2048:2560]; k = w_in[:, :, 2560:2688]; v = w_in[:, :, 2688:2816]
    gates = w_in[:, :, 2816:]
    conv = np.concatenate([np.concatenate([cb[:, :, c * 128:(c + 1) * 128], cc[:, :, c * 128:(c + 1) * 128],
                                           cx[:, :, c * 128:(c + 1) * 128]], -1) for c in range(4)], -1)
    hperm = [h for c in range(4) for h in (c, c + 4)]
    qp = np.concatenate([q[:, :, h * 64:(h + 1) * 64] for h in hperm], -1)
    gp = np.concatenate([gates[:, :, r * D + c * 128: r * D + (c + 1) * 128] for c in range(8) for r in range(3)], -1)
    w_in_p = np.ascontiguousarray(np.concatenate([u, conv, qp, k, v, gp], -1))
    wbr = np.asarray(inp["w_branch"], f).copy()
    wbr[:, 2] = np.concatenate([wbr[:, 2, h * 64:(h + 1) * 64, :] for h in hperm], 1)
    w_br_p = np.ascontiguousarray(np.concatenate([wbr[:, r, :, c * 128:(c + 1) * 128] for c in range(8) for r in range(3)], -1))
    wfi = np.asarray(inp["w_ffn_in"], f)
    w_fi_p = np.ascontiguousarray(np.concatenate([wfi[:, :, o + j * 128: o + (j + 1) * 128] for j in range(FC) for o in (0, FF)], -1))
    sh = {"w_in": w_in_p, "w_glu": np.ascontiguousarray(inp["ssm_w_glu"], f), "w_br": w_br_p,
          "w_out": np.ascontiguousarray(inp["w_out"], f), "w_fi": w_fi_p,
          "w_fo": np.ascontiguousarray(inp["w_ffn_out"], f), "w_pg": np.ascontiguousarray(inp["w_ple_gate"], f),
          "w_pp": np.ascontiguousarray(inp["w_ple_proj"], f)}
    norms = [np.asarray(inp[n], f) for n in ("norm_mix", "norm_ffn", "norm_ple")]
    gl = []
    for nm in norms:
        for l in range(DEPTH):
            gl.append(nm[l].reshape(KC, 128).T)
    gl.append(np.asarray(inp["norm_final"], f).reshape(KC, 128).T)
    sh["gains"] = np.ascontiguousarray(np.concatenate(gl, 1))
    cw = np.asarray(inp["conv_w"], f)
    sh["convw"] = np.ascontiguousarray(cw.reshape(DEPTH, 3, 4, 128).transpose(3, 0, 2, 1).reshape(128, DEPTH * 12))
    sh["dskip"] = np.ascontiguousarray(np.asarray(inp["ssm_d"], f).reshape(DEPTH, 4, 128).transpose(2, 0, 1).reshape(128, DEPTH * 4))
    sh["sinkb"] = np.ascontiguousarray(np.broadcast_to(np.asarray(inp["attn_sinks"], f).reshape(1, DEPTH * 8), (128, DEPTH * 8)))
    rb = np.asarray(inp["rel_bias"], f)
    qi = np.arange(128)[:, None]; kj = np.arange(256)[None, :]
    dist = qi + 128 - kj
    band = (dist >= 0) & (dist < 128)
    bucket = t5_bucket_np(np.clip(dist, 0, 127))
    bias = rb[bucket]
    bm = np.where(band[:, :, None], bias, f(-30000.0)).astype(f)
    sh["biasmask"] = np.ascontiguousarray(bm.transpose(0, 2, 1).reshape(128, 8 * 256))
    sh["identf"] = np.eye(128, dtype=f)
    lre = np.asarray(inp["ssm_lambda_re"], f); lim = np.asarray(inp["ssm_lambda_im"], f)
    ldt = np.asarray(inp["ssm_log_dt"], f)
    ldt_b = np.broadcast_to(ldt[:, :, None], (DEPTH, G, NS))

    def playout(a):
        return a.reshape(DEPTH, 16, 2, NS).transpose(2, 3, 0, 1).reshape(128, DEPTH * 16)
    sh["lamP"] = np.ascontiguousarray(np.concatenate([playout(lre), playout(lim), playout(ldt_b)], 1))
    lr = np.stack([lre.reshape(DEPTH, G * NS), lim.reshape(DEPTH, G * NS), ldt_b.reshape(DEPTH, G * NS)], 1)
    sh["lamR"] = np.ascontiguousarray(np.broadcast_to(lr[:, :, None, :], (DEPTH, 3, 128, G * NS)))
    bpad = np.zeros((DEPTH, 2, 128, G, NS), f)
    cpad = np.zeros((DEPTH, 2, 2, NS, 16, 128), f)
    bsrc = [np.asarray(inp["ssm_b_re"], f), np.asarray(inp["ssm_b_im"], f)]
    csrc = [np.asarray(inp["ssm_c_re"], f), np.asarray(inp["ssm_c_im"], f)]
    for g in range(G):
        g8 = g % 8
        for c in range(2):
            bpad[:, c, g8 * 16:(g8 + 1) * 16, g, :] = bsrc[c][:, g].transpose(0, 2, 1)
            cpad[:, c, g % 2, :, g // 2, g8 * 16:(g8 + 1) * 16] = csrc[c][:, g].transpose(0, 2, 1)
    sh["bpad"] = np.ascontiguousarray(bpad.reshape(DEPTH, 2, 128, G * NS))
    sh["cpad"] = np.ascontiguousarray(cpad.reshape(DEPTH, 2, 128, 16 * 128))
    return sh


_CACHE = {}


def run(inp, T, DEPTH, ncores, debug=None):
    sh = prep_shared(inp, DEPTH)
    x = np.asarray(inp["x"], np.float32); p = np.asarray(inp["p"], np.float32)
    in_maps = []
    for b in range(ncores):
        m = dict(sh)
        m["xT"] = np.ascontiguousarray(x[b].T)
        m["pT"] = np.ascontiguousarray(p[:, b].transpose(0, 2, 1))
        in_maps.append(m)
    key = (T, DEPTH, tuple(sorted(debug)) if debug else None)
    if key not in _CACHE:
        _CACHE[key] = build(T, DEPTH, debug)
    nc, S = _CACHE[key]
    res = run_bass_kernel_spmd(nc, in_maps, core_ids=list(range(ncores)))
    out = np.stack([np.ascontiguousarray(r["outT"].T) for r in res.results], 0)
    return out, res


def kernel(**inputs):
    x = inputs["x"]
    B, T, _ = x.shape
    DEPTH = inputs["w_in"].shape[0]
    out, _ = run(inputs, T, DEPTH, B)
    return out.astype(np.float32)
```

```python
import math
import numpy as np
import concourse.bass as bass
import concourse.mybir as mybir
from concourse.bass_utils import run_bass_kernel_spmd

F32 = mybir.dt.float32
BF16 = mybir.dt.bfloat16
AF = mybir.ActivationFunctionType
ALU = mybir.AluOpType
AX = mybir.AxisListType

D = 1024
KC = 8
NT = 512
BW = 512
FF = 2816
FC = 22
PLE = 256
G = 32
NS = 64
WIN = 5888
ATT_SCALE = 1.0 / 8.0
EPS = 1e-6
ERA = 30000
SLOT = 4096
ATTACH = {'pe', 'dve', 'act', 'pool'}
HF = 11


class Res:
    __slots__ = ("name", "w", "r", "sem", "semcnt", "excl")

    def __init__(self, name, sem=None, excl=False):
        self.name = name
        self.excl = excl
        self.w = None
        self.r = []
        self.sem = sem
        self.semcnt = 0


class Sched:
    def __init__(self, nc):
        self.nc = nc
        self.E = {"pe": nc.tensor, "dve": nc.vector, "act": nc.scalar, "pool": nc.gpsimd, "sp": nc.sync}
        self.nsem = 0
        self.dl = []
        self.sem = {k: self.newsem() for k in self.E}
        self.cnt = {k: 0 for k in self.E}
        self.seen = {k: {} for k in self.E}
        self.ninst = 0

    def newsem(self):
        self.nsem += 1
        return self.nc.alloc_semaphore("sm%d" % self.nsem)

    def dres(self, name):
        r = Res(name, self.newsem())
        self.dl.append(r)
        return r

    def barrier(self, skip=()):
        skip = set(id(x) for x in skip)
        for e, eng in self.E.items():
            seen = self.seen[e]
            for o in self.E:
                if o != e and self.cnt[o] > 0 and seen.get(self.sem[o].name, 0) < self.cnt[o]:
                    eng.wait_ge(self.sem[o], self.cnt[o])
                    seen[self.sem[o].name] = self.cnt[o]
            for r in self.dl:
                if id(r) in skip:
                    continue
                if r.semcnt > 0 and seen.get(r.sem.name, 0) < r.semcnt:
                    eng.wait_ge(r.sem, r.semcnt)
                    seen[r.sem.name] = r.semcnt

    def _need(self, e, R, W):
        need = {}
        seen = self.seen[e]

        def add(dep, raw):
            prod, sem, val = dep
            if prod == e and not raw:
                return
            if prod == "dma":
                val = sem[1].semcnt
                semh = sem[0]
            else:
                semh = sem
            key = semh.name
            if seen.get(key, 0) >= val:
                return
            if key not in need or need[key][1] < val:
                need[key] = (semh, val)
        for r in R:
            if r.w is not None:
                add(r.w, True)
        for w in W:
            if w.w is not None:
                add(w.w, False)
            for x in w.r:
                add(x, False)
        lst = list(need.values())
        for (semh, val) in lst:
            seen[semh.name] = val
        return lst

    def _emit(self, e, lst, ins_fn, attach=True):
        eng = self.E[e]
        attach = attach and (e in ATTACH)
        pre = lst[:-1] if attach else lst
        for (semh, val) in pre:
            eng.wait_ge(semh, val)
            self.ninst += 1
        ins = ins_fn(eng)
        if lst and attach:
            ins._wait_ge(lst[-1][0], lst[-1][1])
        self.ninst += 1
        return ins

    def op(self, e, fn, R=(), W=()):
        W = list(W) + [r for r in R if r.excl]
        R = [r for r in R if not r.excl]
        lst = self._need(e, R, W)
        if self.cnt[e] >= ERA:
            self.sem[e] = self.newsem()
            self.cnt[e] = 0
        ins = self._emit(e, lst, fn)
        self.cnt[e] += 1
        ins.then_inc(self.sem[e], 1)
        tag = (e, self.sem[e], self.cnt[e])
        for r in R:
            r.r.append(tag)
        for w in W:
            w.w = tag
            w.r = []
        return ins

    def dma(self, q, out, in_, R=(), W=(), semres=None):
        sr = semres if semres is not None else W[0]
        lst = self._need(q, R, W)
        ins = self._emit(q, lst, lambda g: g.dma_start(out=out, in_=in_), attach=False)
        sr.semcnt += 16
        ins.then_inc(sr.sem, 16)
        tag = ("dma", (sr.sem, sr), sr.semcnt)
        for r in R:
            r.r.append(tag)
        for w in W:
            w.w = tag
            w.r = []
        return ins


def build(T, DEPTH, debug=None):
    nc = bass.Bass("TRN2", target_bir_lowering=False)
    S = Sched(nc)
    ntiles = T // NT
    dbg = {}

    def din(name, shape, dt=F32):
        return nc.dram_tensor(name, list(shape), dt, kind="ExternalInput").ap()

    def dscr(name, shape, dt=BF16):
        return nc.dram_tensor(name, list(shape), dt, kind="Internal").ap()

    def sb(name, shape, dt=F32):
        return nc.alloc_sbuf_tensor(name, list(shape), dt).ap()

    xT_d = din("xT", [D, T])
    pT_d = din("pT", [DEPTH, PLE, T])
    outT_d = nc.dram_tensor("outT", [D, T], F32, kind="ExternalOutput").ap()
    wnames = {"w_in": (D, WIN), "w_glu": (BW, BW), "w_br": (BW, 3 * D), "w_out": (D, D),
              "w_fi": (D, 2 * FF), "w_fo": (FF, D), "w_pg": (D, D), "w_pp": (PLE, D)}
    w_f = {k: din(k, [DEPTH, v[0], v[1]]) for k, v in wnames.items()}
    w_b = {k: dscr(k + "_b", [DEPTH, v[0], v[1]]) for k, v in wnames.items()}
    w_res = {(k, l): S.dres("wr_%s%d" % (k, l)) for k in wnames for l in range(DEPTH)}
    gains_d = din("gains", [128, (3 * DEPTH + 1) * KC])
    convw_d = din("convw", [128, DEPTH * 4 * 3])
    dskip_d = din("dskip", [128, DEPTH * 4])
    sink_d = din("sinkb", [128, DEPTH * 8])
    bias_d = din("biasmask", [128, 8 * 256])
    identf_d = din("identf", [128, 128])
    lamP_d = din("lamP", [128, 3 * DEPTH * 16])
    lamR_d = din("lamR", [DEPTH, 3, 128, G * NS])
    bpad_d = din("bpad", [DEPTH, 2, 128, 16 * 128])
    cpad_d = din("cpad", [DEPTH, 2, 128, 16 * 128])
    sc_bb = dscr("sc_bb", [DEPTH, 128, 2 * 16 * 128])
    sc_cc = dscr("sc_cc", [DEPTH, 128, 2 * 16 * 128])
    sc_rot = dscr("sc_rot", [DEPTH, 128, 16 * 2 * NT], F32)

    xT = sb("xT_s", [128, KC, NT]); xR = [S.dres("x%d" % c) for c in range(KC)]
    hT = sb("hT_s", [128, KC, NT], BF16); hR = [Res("h%d" % c) for c in range(KC)]
    sq = sb("sq_s", [128, 2, NT], BF16); sqR = [Res("sq0"), Res("sq1")]
    rstd = sb("rstd_s", [128, NT]); rstdR = Res("rstd")
    ones_b = sb("ones_b", [128, 128], BF16); constR = Res("const")
    identf = sb("identf_s", [128, 128]); identb = sb("identb_s", [128, 128], BF16)
    gains = sb("gains_s", [128, (3 * DEPTH + 1) * KC])
    convw = sb("convw_s", [128, DEPTH * 12])
    dskip = sb("dskip_s", [128, DEPTH * 4])
    sinkb = sb("sink_s", [128, DEPTH * 8])
    biasm = sb("bias_s", [128, 8, 256])
    epsb = sb("eps_s", [128, 1])
    smallR = S.dres("small")
    uT = sb("uT_s", [128, 4, NT], BF16); uR = [Res("u%d" % c) for c in range(4)]
    ysb = sb("ysb_s", [128, NT]); ysbR = Res("ysb")
    ygT = sb("ygT_s", [128, 4, NT], BF16); ygR = [Res("yg%d" % c) for c in range(4)]
    qz = sb("qz_s", [128, 8, NT], BF16); qzR = [Res("qz%d" % h) for h in range(8)]
    kT = sb("kT_s", [128, DEPTH, NT + 128], BF16); kR = [Res("k%d" % l) for l in range(DEPTH)]
    vz = sb("vz_s", [128, DEPTH, 5 * 2 * 128], BF16); vR = [Res("v%d" % l) for l in range(DEPTH)]
    vcv = sb("vcv_s", [128, 2, NT + 2]); vcvR = [Res("vcv0"), Res("vcv1")]
    vhist = sb("vhist_s", [128, DEPTH * 4 * 2]); vhR = Res("vhist")
    acc = sb("acc_s", [128, 2, NT]); accR = [Res("acc0"), Res("acc1")]
    tmpA = sb("tmpA_s", [128, 2, NT]); tmpAR = [Res("tA0"), Res("tA1")]
    tmpB = sb("tmpB_s", [128, 2, NT]); tmpBR = [S.dres("tB0"), S.dres("tB1")]
    pTb = sb("pTb_s", [128, 2, NT], BF16); pTbR = S.dres("pTb")
    ARENA = 14336
    arena = sb("arena", [128, ARENA])
    NSLOT = 4
    wsl = [arena[:, i * 2048:(i + 1) * 2048].bitcast(BF16) for i in range(NSLOT)]
    wslR = [S.dres("wsl%d" % i) for i in range(NSLOT)]
    gT = arena[:, 8192:8192 + 2816].bitcast(BF16).rearrange("p (c t) -> p c t", t=NT)
    gR = [Res("g%d" % c) for c in range(HF)]
    mergedT = gT; mgR = gR
    ybr = [arena[:, 11008 + r * 1024:11008 + (r + 1) * 1024].bitcast(BF16).rearrange("p (c t) -> p c t", t=NT) for r in range(3)]
    ybrR = [[Res("ybr%d_%d" % (r, c)) for c in range(4)] for r in range(3)]
    bbS = sb("bbS", [128, 2, 16, 128], BF16); ccS = sb("ccS", [128, 2, 16, 128], BF16)
    bbR = S.dres("bbS"); ccR = S.dres("ccS")
    rot = sb("rot_s", [128, 2, 2, NT]); rotR = [S.dres("rot0"), S.dres("rot1")]
    rho = sb("rho_s", [128, DEPTH * 16])
    carry = sb("carry_s", [128, DEPTH * 16 * 2]); carryR = Res("carry")
    cst = sb("cst_s", [128, 4]); cstR = Res("cst")
    bre = sb("bre_s", [128, NT]); breR = Res("bre")
    bim = sb("bim_s", [128, NT]); bimR = Res("bim")
    t1 = sb("t1_s", [128, NT]); t1R = Res("t1")
    t2 = sb("t2_s", [128, NT]); t2R = Res("t2")
    t3 = sb("t3_s", [128, NT]); t3R = Res("t3")
    t4 = sb("t4_s", [128, NT]); t4R = Res("t4")
    gre = sb("gre_s", [128, NT]); greR = Res("gre")
    gim = sb("gim_s", [128, NT]); gimR = Res("gim")
    hre = sb("hre_s", [128, 2, NT], BF16); hreR = [Res("hre0"), Res("hre1")]
    nhi = sb("nhi_s", [128, 2, NT], BF16); nhiR = [Res("nhi0"), Res("nhi1")]
    s_sb = sb("ssb_s", [128, 2, 256]); ssbR = [Res("ssb0"), Res("ssb1")]
    p_f = sb("pf_s", [128, 2, 256]); pfR = [Res("pf0"), Res("pf1")]
    p_b = sb("pb_s", [128, 2, 256], BF16); pbR = [Res("pb0"), Res("pb1")]
    pTs = sb("pTs_s", [128, 2, 256], BF16); pTsR = [Res("pTs0"), Res("pTs1")]
    stat = sb("stat_s", [128, 2, 8]); statR = [Res("st0"), Res("st1")]
    NPS = 4
    psb = [nc.alloc_psum_tensor("ps%d" % i, [128, 512], F32).ap() for i in range(NPS)]
    psR = [Res("ps%d" % i, excl=True) for i in range(NPS)]
    psy = nc.alloc_psum_tensor("psy", [128, 512], F32).ap(); psyR = Res("psy", excl=True)
    pss = [nc.alloc_psum_tensor("pss%d" % i, [128, 512], F32).ap() for i in range(2)]
    pssR = [Res("pss0", excl=True), Res("pss1", excl=True)]
    pst_t = nc.alloc_psum_tensor("pst", [128, 1024], BF16).ap()
    pst = [pst_t[:, 0:256], pst_t[:, 0:256]]
    pstR = [Res("pst0", excl=True)] * 2
    st = {"ps": 0, "slot": 0, "ev": 0, "ai": 0}

    def psum():
        i = st["ps"]; st["ps"] = (i + 1) % NPS
        return psb[i], psR[i]

    def evac_eng():
        st["ev"] ^= 1
        return "act" if st["ev"] else "dve"

    def copy(e, out, in_, R, W):
        if e == "act":
            return S.op("act", lambda g: g.activation(out=out, in_=in_, func=AF.Copy), R, W)
        return S.op(e, lambda g: g.tensor_copy(out=out, in_=in_), R, W)

    def dump(name, ap, R, shape, dt=F32):
        if debug is None or name not in debug:
            return
        d = nc.dram_tensor("dbg_" + name, list(shape), dt, kind="ExternalOutput").ap()
        r = S.dres("dbg_" + name)
        S.dma("sp", d, ap, R=R, W=[r])
        dbg[name] = r

    def tt(e, out, a, b, op, R, W):
        return S.op(e, lambda g: g.tensor_tensor(out=out, in0=a, in1=b, op=op), R, W)

    def ts(e, out, a, s1, s2, op0, op1, R, W):
        return S.op(e, lambda g: g.tensor_scalar(out=out, in0=a, scalar1=s1, scalar2=s2, op0=op0, op1=op1), R, W)

    def act(out, in_, func, R, W, bias=None, scale=None, accum=None):
        kw = {}
        if bias is not None:
            kw["bias"] = bias
        if scale is not None:
            kw["scale"] = scale
        if accum is not None:
            kw["accum_out"] = accum
        return S.op("act", lambda g: g.activation(out=out, in_=in_, func=func, **kw), R, W)

    for (dst, src) in [(gains, gains_d), (convw, convw_d), (dskip, dskip_d), (sinkb, sink_d),
                       (biasm.rearrange("p h j -> p (h j)"), bias_d), (identf, identf_d)]:
        S.dma("sp", dst, src, W=[smallR])
    S.op("dve", lambda g: g.memset(ones_b, 1.0), W=[constR])
    S.op("dve", lambda g: g.memset(epsb, EPS), W=[constR])
    S.op("dve", lambda g: g.tensor_copy(out=identb, in_=identf), R=[smallR], W=[constR])
    S.op("pool", lambda g: g.memset(qz.rearrange("p h t -> p (h t)"), 0.0), W=qzR)
    S.op("pool", lambda g: g.memset(kT.rearrange("p l t -> p (l t)"), 0.0), W=kR)
    S.op("pool", lambda g: g.memset(vz.rearrange("p l t -> p (l t)"), 0.0), W=vR)
    S.op("pool", lambda g: g.memset(vhist, 0.0), W=[vhR])
    S.op("pool", lambda g: g.memset(carry, 0.0), W=[carryR])

    for l in range(DEPTH):
        for k, (K, N) in wnames.items():
            rows = max(128, (1 << 20) // N // 128 * 128)
            r0 = 0
            while r0 < K:
                r1 = min(K, r0 + rows)
                S.dma("pool", w_b[k][l, r0:r1, :], w_f[k][l, r0:r1, :], W=[w_res[(k, l)]])
                r0 = r1
        r2 = S.dres("ccscr%d" % l)
        S.dma("pool", sc_cc[l].rearrange("p (c n) -> c p n", c=2), cpad_d[l], W=[r2])
        w_res[("cc", l)] = r2

    lamP = sb("lamP_s", [128, 3, DEPTH * 16]); lamPR = S.dres("lamP")
    S.dma("sp", lamP.rearrange("p a b -> p (a b)"), lamP_d, W=[lamPR])
    QN = 512
    LR = arena[:, 0:1536].rearrange("p (a n) -> p a n", a=3); LRR = S.dres("LR")
    wk = [arena[:, 1536 + i * 512:1536 + (i + 1) * 512] for i in range(8)]
    wkR = [Res("wk%d" % i) for i in range(8)]
    bpS = arena[:, 5632:6656].rearrange("p (a n) -> p a n", a=2); bpR = S.dres("bpS")
    bbo = arena[:, 6656:7168].bitcast(BF16).rearrange("p (a n) -> p a n", a=2); bboR = S.dres("bbo")
    rt = arena[:, 7168:11264].rearrange("p (a b c) -> p a b c", a=4, b=2); rtR = S.dres("rt")
    rtm = [arena[:, 11264 + i * 1024:11264 + (i + 1) * 1024].rearrange("p (a n) -> p a n", a=4) for i in range(3)]
    rtmR = [Res("rtm%d" % i) for i in range(3)]

    def cexp_unit(ang, c_out, s_out, tmp, RA, Rc, Rs, Rt, n_sq=3):
        sc = 1.0 / (1 << n_sq)
        act(s_out, ang, AF.Sin, [RA], [Rs], scale=sc)
        act(tmp, ang, AF.Sin, [RA], [Rt], scale=sc * 0.5)
        tt("dve", tmp, tmp, tmp, ALU.mult, [Rt], [Rt])
        ts("dve", c_out, tmp, -2.0, 1.0, ALU.mult, ALU.add, [Rt], [Rc])
        for _ in range(n_sq):
            tt("dve", tmp, c_out, s_out, ALU.mult, [Rc, Rs], [Rt])
            tt("dve", c_out, c_out, c_out, ALU.mult, [Rc], [Rc])
            tt("dve", s_out, s_out, s_out, ALU.mult, [Rs], [Rs])
            tt("dve", c_out, c_out, s_out, ALU.subtract, [Rc, Rs], [Rc])
            ts("dve", s_out, tmp, 2.0, None, ALU.mult, ALU.bypass, [Rt], [Rs])

    NL = DEPTH * 16
    pw = [sb("pw%d" % i, [128, NL]) for i in range(6)]
    pwR = [Res("pw%d" % i) for i in range(6)]
    rhoR = Res("rho")
    act(pw[0], lamP[:, 2, :], AF.Exp, [lamPR], [pwR[0]])
    tt("dve", pw[1], lamP[:, 0, :], pw[0], ALU.mult, [lamPR, pwR[0]], [pwR[1]])
    act(rho, pw[1], AF.Exp, [pwR[1]], [rhoR])
    tt("dve", pw[2], lamP[:, 1, :], pw[0], ALU.mult, [lamPR, pwR[0]], [pwR[2]])
    cexp_unit(pw[2], pw[3], pw[4], pw[5], pwR[2], pwR[3], pwR[4], pwR[5])
    for l in range(DEPTH):
        rotres_l = S.dres("rotscr%d" % l)
        w_res[("rot", l)] = rotres_l
        for gq in range(4):
            cs = pw[3][:, l * 16 + gq * 4:l * 16 + gq * 4 + 4]
            sn = pw[4][:, l * 16 + gq * 4:l * 16 + gq * 4 + 4]
            S.op("dve", lambda g: g.tensor_copy(out=rt[:, :, 0, 0], in_=cs), [pwR[3]], [rtR])
            ts("dve", rt[:, :, 1, 0], sn, -1.0, None, ALU.mult, ALU.bypass, [pwR[4]], [rtR])
            n = 1
            while n < NT:
                fc = rt[:, :, 0, 0:n]; fs = rt[:, :, 1, 0:n]
                mc = rt[:, :, 0, n - 1:n].to_broadcast([128, 4, n]); ms = rt[:, :, 1, n - 1:n].to_broadcast([128, 4, n])
                a0 = rtm[0][:, :, 0:n]; a1 = rtm[1][:, :, 0:n]; a2 = rtm[2][:, :, 0:n]
                tt("dve", a0, fc, mc, ALU.mult, [rtR], [rtmR[0]])
                tt("dve", a1, fs, ms, ALU.mult, [rtR], [rtmR[1]])
                tt("dve", a2, fc, ms, ALU.mult, [rtR], [rtmR[2]])
                tt("dve", rt[:, :, 0, n:2 * n], a0, a1, ALU.subtract, [rtmR[0], rtmR[1]], [rtR])
                tt("dve", a0, fs, mc, ALU.mult, [rtR], [rtmR[0]])
                tt("dve", rt[:, :, 1, n:2 * n], a2, a0, ALU.add, [rtmR[2], rtmR[0]], [rtR])
                n *= 2
            S.dma("sp", sc_rot[l, :, gq * 4 * 2 * NT:(gq + 1) * 4 * 2 * NT], rt.rearrange("p a b c -> p (a b c)"),
                  R=[rtR], W=[rotres_l])
        r1 = S.dres("bbscr%d" % l)
        w_res[("bb", l)] = r1
        for qd in range(4):
            for a in range(3):
                S.dma("sp", LR[:, a, :], lamR_d[l, a, :, qd * QN:(qd + 1) * QN], W=[LRR])
            for c in range(2):
                S.dma("sp", bpS[:, c, :], bpad_d[l, c, :, qd * QN:(qd + 1) * QN], W=[bpR])
            lre = LR[:, 0, :]; lim = LR[:, 1, :]
            act(wk[0], LR[:, 2, :], AF.Exp, [LRR], [wkR[0]])
            tt("dve", wk[1], lre, wk[0], ALU.mult, [LRR, wkR[0]], [wkR[1]])
            act(wk[1], wk[1], AF.Exp, [wkR[1]], [wkR[1]])
            tt("dve", wk[2], lim, wk[0], ALU.mult, [LRR, wkR[0]], [wkR[2]])
            cexp_unit(wk[2], wk[3], wk[4], wk[5], wkR[2], wkR[3], wkR[4], wkR[5])
            tt("dve", wk[3], wk[3], wk[1], ALU.mult, [wkR[3], wkR[1]], [wkR[3]])
            tt("dve", wk[4], wk[4], wk[1], ALU.mult, [wkR[4], wkR[1]], [wkR[4]])
            ts("dve", wk[3], wk[3], -1.0, None, ALU.add, ALU.bypass, [wkR[3]], [wkR[3]])
            tt("dve", wk[0], lre, lre, ALU.mult, [LRR], [wkR[0]])
            tt("dve", wk[1], lim, lim, ALU.mult, [LRR], [wkR[1]])
            tt("dve", wk[0], wk[0], wk[1], ALU.add, [wkR[0], wkR[1]], [wkR[0]])
            S.op("dve", lambda g: g.reciprocal(out=wk[0], in_=wk[0]), [wkR[0]], [wkR[0]])
            tt("dve", wk[1], wk[3], lre, ALU.mult, [wkR[3], LRR], [wkR[1]])
            tt("dve", wk[2], wk[4], lim, ALU.mult, [wkR[4], LRR], [wkR[2]])
            tt("dve", wk[1], wk[1], wk[2], ALU.add, [wkR[1], wkR[2]], [wkR[1]])
            tt("dve", wk[1], wk[1], wk[0], ALU.mult, [wkR[1], wkR[0]], [wkR[1]])
            tt("dve", wk[2], wk[4], lre, ALU.mult, [wkR[4], LRR], [wkR[2]])
            tt("dve", wk[5], wk[3], lim, ALU.mult, [wkR[3], LRR], [wkR[5]])
            tt("dve", wk[2], wk[2], wk[5], ALU.subtract, [wkR[2], wkR[5]], [wkR[2]])
            tt("dve", wk[2], wk[2], wk[0], ALU.mult, [wkR[2], wkR[0]], [wkR[2]])
            tt("dve", wk[3], wk[1], bpS[:, 0, :], ALU.mult, [wkR[1], bpR], [wkR[3]])
            tt("dve", wk[4], wk[2], bpS[:, 1, :], ALU.mult, [wkR[2], bpR], [wkR[4]])
            tt("dve", bbo[:, 0, :], wk[3], wk[4], ALU.subtract, [wkR[3], wkR[4]], [bboR])
            tt("dve", wk[3], wk[1], bpS[:, 1, :], ALU.mult, [wkR[1], bpR], [wkR[3]])
            tt("dve", wk[4], wk[2], bpS[:, 0, :], ALU.mult, [wkR[2], bpR], [wkR[4]])
            tt("dve", bbo[:, 1, :], wk[3], wk[4], ALU.add, [wkR[3], wkR[4]], [bboR])
            dst = sc_bb[l].rearrange("p (c n) -> p c n", c=2)[:, :, qd * QN:(qd + 1) * QN]
            S.dma("sp", dst, bbo, R=[bboR], W=[r1])
    S.barrier(skip=[w_res[(k, l)] for k in wnames for l in range(DEPTH)])

    def wload(k, l, r0, nkc, c0, ncols):
        i = st["slot"]; st["slot"] = (i + 1) % NSLOT
        assert nkc * ncols <= SLOT
        src = w_b[k][l, r0:r0 + nkc * 128, c0:c0 + ncols].rearrange("(kc p) n -> p kc n", p=128)
        dst = wsl[i][:, 0:nkc * ncols].rearrange("p (kc n) -> p kc n", n=ncols)
        S.dma("sp", dst, src, R=[w_res[(k, l)]], W=[wslR[i]])
        return dst, wslR[i]

    def group(out_ap, outR, lhs_list, rhs_list, R):
        n = len(lhs_list)
        for i in range(n):
            S.op("pe", lambda g, i=i: g.matmul(out_ap, lhsT=lhs_list[i], rhs=rhs_list[i], start=(i == 0), stop=(i == n - 1)),
                 R, [outR])

    def norm():
        ps, pR = psum()
        for c in range(KC):
            j = c % 2
            act(sq[:, j, :], xT[:, c, :], AF.Square, [xR[c]], [sqR[j]])
            S.op("pe", lambda g, c=c, j=j: g.matmul(ps, lhsT=ones_b, rhs=sq[:, j, :], start=(c == 0), stop=(c == KC - 1)),
                 [constR, sqR[j]], [pR])
        act(rstd, ps, AF.Sqrt, [pR, constR], [rstdR], bias=epsb[:, 0:1], scale=1.0 / D)
        S.op("dve", lambda g: g.reciprocal(out=rstd, in_=rstd), [rstdR], [rstdR])

    def norm_apply(gidx):
        for c in range(KC):
            S.op("dve", lambda g, c=c: g.scalar_tensor_tensor(
                out=hT[:, c, :], in0=xT[:, c, :], scalar=gains[:, gidx * KC + c:gidx * KC + c + 1], in1=rstd,
                op0=ALU.mult, op1=ALU.mult), [xR[c], rstdR, smallR], [hR[c]])

    hrhs = [hT[:, c, :] for c in range(KC)]

    for ti in range(ntiles):
        t0 = ti * NT
        for c in range(KC):
            S.dma("sp", xT[:, c, :], xT_d[c * 128:(c + 1) * 128, t0:t0 + NT], W=[xR[c]])
        for l in range(DEPTH):
            d0 = (ti == 0 and l == 0)
            norm()
            norm_apply(0 * DEPTH + l)
            if d0:
                dump("h0", hT.rearrange("p c t -> p (c t)"), hR, [128, KC * NT], BF16)
            S.dma("pool", bbS.rearrange("p a b c -> p (a b c)"), sc_bb[l], R=[w_res[("bb", l)]], W=[bbR])
            S.dma("pool", ccS.rearrange("p a b c -> p (a b c)"), sc_cc[l], R=[w_res[("cc", l)]], W=[ccR])
            wv, wR_ = wload("w_in", l, 0, KC, 0, 512)
            for mo in range(4):
                ps, pR = psum()
                group(ps, pR, [wv[:, kc, mo * 128:(mo + 1) * 128] for kc in range(KC)], hrhs, [wR_] + hR)
                copy(evac_eng(), uT[:, mo, :], ps, [pR], [uR[mo]])
            for c in range(4):
                wv, wR_ = wload("w_in", l, 0, KC, 512 + c * 384, 384)
                j = c % 2
                psB, pRB = psum(); psC, pRC = psum(); psX, pRX = psum()
                group(psC, pRC, [wv[:, kc, 128:256] for kc in range(KC)], hrhs, [wR_] + hR)
                group(psX, pRX, [wv[:, kc, 256:384] for kc in range(KC)], hrhs, [wR_] + hR)
                group(psB, pRB, [wv[:, kc, 0:128] for kc in range(KC)], hrhs, [wR_] + hR)
                copy("act", tmpA[:, j, :], psC, [pRC], [tmpAR[j]])
                hi = (l * 4 + c) * 2
                copy("pool", vcv[:, j, 0:2], vhist[:, hi:hi + 2], [vhR], [vcvR[j]])
                tt("dve", vcv[:, j, 2:NT + 2], tmpA[:, j, :], psX, ALU.mult, [tmpAR[j], pRX], [vcvR[j]])
                copy("pool", vhist[:, hi:hi + 2], vcv[:, j, NT:NT + 2], [vcvR[j]], [vhR])
                wi = (l * 4 + c) * 3
                act(acc[:, j, :], vcv[:, j, 0:NT], AF.Copy, [vcvR[j], smallR], [accR[j]], scale=convw[:, wi:wi + 1])
                S.op("dve", lambda g: g.scalar_tensor_tensor(out=acc[:, j, :], in0=vcv[:, j, 1:NT + 1], scalar=convw[:, wi + 1:wi + 2],
                                                             in1=acc[:, j, :], op0=ALU.mult, op1=ALU.add), [vcvR[j], accR[j], smallR], [accR[j]])
                S.op("dve", lambda g: g.scalar_tensor_tensor(out=acc[:, j, :], in0=vcv[:, j, 2:NT + 2], scalar=convw[:, wi + 2:wi + 3],
                                                             in1=acc[:, j, :], op0=ALU.mult, op1=ALU.add), [vcvR[j], accR[j], smallR], [accR[j]])
                tt("dve", ybr[1][:, c, :], acc[:, j, :], psB, ALU.mult, [accR[j], pRB], [ybrR[1][c]])
            wv, wR_ = wload("w_in", l, 0, KC, 2048, 512)
            for c in range(4):
                ps, pR = psum()
                group(ps, pR, [wv[:, kc, c * 128:(c + 1) * 128] for kc in range(KC)], hrhs, [wR_] + hR)
                copy("act", qz[0:64, c, :], ps[0:64, :], [pR], [qzR[c]])
                copy("dve", qz[64:128, c + 4, :], ps[64:128, :], [pR], [qzR[c + 4]])
            wv, wR_ = wload("w_in", l, 0, KC, 2560, 256)
            ps, pR = psum()
            group(ps, pR, [wv[:, kc, 0:128] for kc in range(KC)], hrhs, [wR_] + hR)
            copy("act", kT[:, l, 128:128 + NT], ps, [pR], [kR[l]])
            ps, pR = psum()
            for blk in range(4):
                group(ps[:, blk * 128:(blk + 1) * 128], pR, [hT[:, kc, blk * 128:(blk + 1) * 128] for kc in range(KC)],
                      [wv[:, kc, 128:256] for kc in range(KC)], [wR_] + hR)
            vzl = vz[:, l, :].rearrange("p (b k d) -> p b k d", b=5, k=2)
            psv = ps.rearrange("p (b d) -> p b d", b=4)
            copy("dve", vzl[:, 1:5, 0, 0:64], psv[:, :, 0:64], [pR], [vR[l]])
            copy("act", vzl[:, 1:5, 1, 64:128], psv[:, :, 64:128], [pR], [vR[l]])
            def attn_pair(nb, c):
                first = (ti == 0 and nb == 0)
                nk = 128 if first else 256
                k0 = nb * 128 + (128 if first else 0)
                pso, pRo = psum()
                for jh in range(2):
                    h = c + 4 * jh
                    a = st["ai"] % 2; st["ai"] += 1
                    S.op("pe", lambda g: g.matmul(pss[a][:, 0:nk], lhsT=qz[:, h, nb * 128:(nb + 1) * 128], rhs=kT[:, l, k0:k0 + nk],
                                                  start=True, stop=True), [qzR[h], kR[l]], [pssR[a]])
                    S.op("dve", lambda g: g.scalar_tensor_tensor(out=s_sb[:, a, 0:nk], in0=pss[a][:, 0:nk], scalar=ATT_SCALE,
                                                                 in1=biasm[:, h, 256 - nk:256], op0=ALU.mult, op1=ALU.add),
                         [pssR[a], smallR], [ssbR[a]])
                    sk = sinkb[:, l * 8 + h:l * 8 + h + 1]
                    S.op("dve", lambda g: g.reduce_max(out=stat[:, a, 0:1], in_=s_sb[:, a, 0:nk], axis=AX.X), [ssbR[a]], [statR[a]])
                    ts("dve", stat[:, a, 1:2], stat[:, a, 0:1], sk, -1.0, ALU.max, ALU.mult, [statR[a], smallR], [statR[a]])
                    act(p_f[:, a, 0:nk], s_sb[:, a, 0:nk], AF.Exp, [ssbR[a], statR[a]], [pfR[a], statR[a]], bias=stat[:, a, 1:2],
                        accum=stat[:, a, 2:3])
                    act(stat[:, a, 3:4], stat[:, a, 1:2], AF.Exp, [statR[a], smallR], [statR[a]], bias=sk)
                    tt("dve", stat[:, a, 4:5], stat[:, a, 2:3], stat[:, a, 3:4], ALU.add, [statR[a]], [statR[a]])
                    S.op("dve", lambda g: g.reciprocal(out=stat[:, a, 5:6], in_=stat[:, a, 4:5]), [statR[a]], [statR[a]])
                    act(p_b[:, a, 0:nk], p_f[:, a, 0:nk], AF.Copy, [pfR[a], statR[a]], [pbR[a]], scale=stat[:, a, 5:6])
                    for kb in range(nk // 128):
                        S.op("pe", lambda g, kb=kb: g.transpose(pst[a][:, kb * 128:(kb + 1) * 128], p_b[:, a, kb * 128:(kb + 1) * 128], identb),
                             [pbR[a], constR], [pstR[a]])
                    copy("act", pTs[:, a, 0:nk], pst[a][:, 0:nk], [pstR[a]], [pTsR[a]])
                    for kb in range(nk // 128):
                        blk = nb + kb + (1 if first else 0)
                        S.op("pe", lambda g, kb=kb, blk=blk: g.matmul(pso[:, 0:128], lhsT=vzl[:, blk, jh, :], rhs=pTs[:, a, kb * 128:(kb + 1) * 128],
                                                                      start=(jh == 0 and kb == 0), stop=(jh == 1 and kb == nk // 128 - 1)),
                             [vR[l], pTsR[a]], [pRo])
                copy(evac_eng(), ybr[2][:, c, nb * 128:(nb + 1) * 128], pso[:, 0:128], [pRo], [ybrR[2][c]])
            for gp in range(16):
                ch = gp // 4
                j = gp % 2
                S.dma("pool", rot[:, j].rearrange("p a t -> p (a t)"), sc_rot[l, :, gp * 2 * NT:(gp + 1) * 2 * NT],
                      R=[w_res[("rot", l)]], W=[rotR[j]])
                Fc = rot[:, j, 0, :]; Fs = rot[:, j, 1, :]
                ps_r, pRr = psum()
                S.op("pe", lambda g: g.matmul(ps_r, lhsT=bbS[:, 0, gp, :], rhs=uT[:, ch, :], start=True, stop=True),
                     [bbR, uR[ch]], [pRr])
                ps_i, pRi = psum()
                S.op("pe", lambda g: g.matmul(ps_i, lhsT=bbS[:, 1, gp, :], rhs=uT[:, ch, :], start=True, stop=True),
                     [bbR, uR[ch]], [pRi])
                copy("act", bre, ps_r, [pRr], [breR])
                copy("act", bim, ps_i, [pRi], [bimR])
                tt("dve", t1, bre, Fc, ALU.mult, [breR, rotR[j]], [t1R])
                tt("pool", t2, bim, Fs, ALU.mult, [bimR, rotR[j]], [t2R])
                tt("dve", gre, t1, t2, ALU.subtract, [t1R, t2R], [greR])
                tt("pool", t3, bre, Fs, ALU.mult, [breR, rotR[j]], [t3R])
                tt("dve", t4, bim, Fc, ALU.mult, [bimR, rotR[j]], [t4R])
                tt("pool", gim, t3, t4, ALU.add, [t3R, t4R], [gimR])
                rb = rho[:, l * 16 + gp:l * 16 + gp + 1].to_broadcast([128, NT])
                ci = (l * 16 + gp) * 2
                S.op("dve", lambda g: g.tensor_tensor_scan(out=t1, data0=rb, data1=gre, initial=carry[:, ci:ci + 1],
                                                           op0=ALU.mult, op1=ALU.add), [greR, rhoR, carryR], [t1R])
                S.op("dve", lambda g: g.tensor_tensor_scan(out=t2, data0=rb, data1=gim, initial=carry[:, ci + 1:ci + 2],
                                                           op0=ALU.mult, op1=ALU.add), [gimR, rhoR, carryR], [t2R])
                L1 = slice(NT - 1, NT)
                tt("pool", cst[:, 0:1], t1[:, L1], Fc[:, L1], ALU.mult, [t1R, rotR[j]], [cstR])
                tt("pool", cst[:, 1:2], t2[:, L1], Fs[:, L1], ALU.mult, [t2R, rotR[j]], [cstR])
                tt("pool", cst[:, 2:3], t2[:, L1], Fc[:, L1], ALU.mult, [t2R, rotR[j]], [cstR])
                tt("pool", cst[:, 3:4], t1[:, L1], Fs[:, L1], ALU.mult, [t1R, rotR[j]], [cstR])
                tt("pool", carry[:, ci:ci + 1], cst[:, 0:1], cst[:, 1:2], ALU.add, [cstR], [carryR])
                tt("pool", carry[:, ci + 1:ci + 2], cst[:, 2:3], cst[:, 3:4], ALU.subtract, [cstR], [carryR])
                attn_pair(gp // 4, gp % 4)
                tt("dve", t3, t1, Fc, ALU.mult, [t1R, rotR[j]], [t3R])
                tt("pool", t4, t2, Fs, ALU.mult, [t2R, rotR[j]], [t4R])
                tt("dve", hre[:, j, :], t3, t4, ALU.add, [t3R, t4R], [hreR[j]])
                tt("pool", gre, t1, Fs, ALU.mult, [t1R, rotR[j]], [greR])
                tt("dve", gim, t2, Fc, ALU.mult, [t2R, rotR[j]], [gimR])
                tt("dve", nhi[:, j, :], gre, gim, ALU.subtract, [greR, gimR], [nhiR[j]])
                S.op("pe", lambda g: g.matmul(psy, lhsT=ccS[:, 0, gp, :], rhs=hre[:, j, :], start=(gp % 4 == 0), stop=False),
                     [ccR, hreR[j]], [psyR])
                S.op("pe", lambda g: g.matmul(psy, lhsT=ccS[:, 1, gp, :], rhs=nhi[:, j, :], start=False, stop=(gp % 4 == 3)),
                     [ccR, nhiR[j]], [psyR])
                if gp % 4 == 3:
                    S.op("dve", lambda g: g.scalar_tensor_tensor(out=ysb, in0=uT[:, ch, :], scalar=dskip[:, l * 4 + ch:l * 4 + ch + 1],
                                                                 in1=psy, op0=ALU.mult, op1=ALU.add), [uR[ch], psyR, smallR], [ysbR])
                    if d0 and ch == 0:
                        dump("yssm0", ysb, [ysbR], [128, NT])
                    act(t3, ysb, AF.Square, [ysbR], [t3R])
                    ts("dve", t3, t3, 0.044715, 1.0, ALU.mult, ALU.add, [t3R], [t3R])
                    tt("pool", t3, t3, ysb, ALU.mult, [t3R, ysbR], [t3R])
                    act(t4, t3, AF.Sigmoid, [t3R], [t4R], scale=1.5957691216057308)
                    tt("dve", ygT[:, ch, :], ysb, t4, ALU.mult, [ysbR, t4R], [ygR[ch]])
            wv, wR_ = wload("w_glu", l, 0, 4, 0, 512)
            for mo in range(4):
                ps, pR = psum()
                group(ps, pR, [wv[:, kc, mo * 128:(mo + 1) * 128] for kc in range(4)], [ygT[:, kc, :] for kc in range(4)], [wR_] + ygR)
                j = mo % 2
                act(tmpA[:, j, :], ps, AF.Sigmoid, [pR], [tmpAR[j]])
                tt("dve", ybr[0][:, mo, :], ygT[:, mo, :], tmpA[:, j, :], ALU.mult, [ygR[mo], tmpAR[j]], [ybrR[0][mo]])
            copy("pool", kT[:, l, 0:128], kT[:, l, NT:NT + 128], [kR[l]], [kR[l]])
            copy("pool", vz[:, l, 0:256], vz[:, l, 4 * 256:5 * 256], [vR[l]], [vR[l]])
            if d0:
                dump("yconv", ybr[1].rearrange("p c t -> p (c t)"), ybrR[1], [128, 4 * NT], BF16)
                dump("yattn", ybr[2].rearrange("p c t -> p (c t)"), ybrR[2], [128, 4 * NT], BF16)
                dump("yssm", ybr[0].rearrange("p c t -> p (c t)"), ybrR[0], [128, 4 * NT], BF16)
            for c in range(KC):
                wg, wgR = wload("w_in", l, 0, KC, 2816 + c * 384, 384)
                wb_, wbR = wload("w_br", l, 0, 4, c * 384, 384)
                j = c % 2
                for r in range(3):
                    col = r * 128
                    psg, pRg = psum()
                    group(psg, pRg, [wg[:, kc, col:col + 128] for kc in range(KC)], hrhs, [wgR] + hR)
                    psbr, pRb = psum()
                    group(psbr, pRb, [wb_[:, kc, col:col + 128] for kc in range(4)], [ybr[r][:, kc, :] for kc in range(4)], [wbR] + ybrR[r])
                    act(tmpA[:, j, :], psg, AF.Sigmoid, [pRg], [tmpAR[j]])
                    if r == 0:
                        tt("dve", acc[:, j, :], tmpA[:, j, :], psbr, ALU.mult, [tmpAR[j], pRb], [accR[j]])
                    else:
                        tt("dve", tmpB[:, j, :], tmpA[:, j, :], psbr, ALU.mult, [tmpAR[j], pRb], [tmpBR[j]])
                        if r == 1:
                            tt("pool", acc[:, j, :], acc[:, j, :], tmpB[:, j, :], ALU.add, [accR[j], tmpBR[j]], [accR[j]])
                        else:
                            tt("pool", mergedT[:, c, :], acc[:, j, :], tmpB[:, j, :], ALU.add, [accR[j], tmpBR[j]], [mgR[c]])
            mrhs = [mergedT[:, kc, :] for kc in range(KC)]
            for pc in range(2):
                wv, wR_ = wload("w_out", l, 0, KC, pc * 512, 512)
                for mo in range(4):
                    c = pc * 4 + mo
                    ps, pR = psum()
                    group(ps, pR, [wv[:, kc, mo * 128:(mo + 1) * 128] for kc in range(KC)], mrhs, [wR_] + mgR[0:KC])
                    tt("dve", xT[:, c, :], xT[:, c, :], ps, ALU.add, [xR[c], pR], [xR[c]])
            if d0:
                dump("x1", xT.rearrange("p c t -> p (c t)"), xR, [128, KC * NT])
            norm()
            norm_apply(1 * DEPTH + l)
            for half in range(2):
                for pc in range(HF // 2 + 1):
                    njj = 2 if pc < HF // 2 else 1
                    jf0 = half * HF + pc * 2
                    wv, wR_ = wload("w_fi", l, 0, KC, jf0 * 256, njj * 256)
                    for jj in range(njj):
                        jl = pc * 2 + jj
                        j = jl % 2
                        psa, pRa = psum(); psb_, pRb = psum()
                        group(psa, pRa, [wv[:, kc, jj * 256:jj * 256 + 128] for kc in range(KC)], hrhs, [wR_] + hR)
                        group(psb_, pRb, [wv[:, kc, jj * 256 + 128:jj * 256 + 256] for kc in range(KC)], hrhs, [wR_] + hR)
                        act(tmpA[:, j, :], psa, AF.Sigmoid, [pRa], [tmpAR[j]])
                        tt("dve", tmpA[:, j, :], tmpA[:, j, :], psa, ALU.mult, [tmpAR[j], pRa], [tmpAR[j]])
                        tt("dve", gT[:, jl, :], tmpA[:, j, :], psb_, ALU.mult, [tmpAR[j], pRb], [gR[jl]])
                grhs = [gT[:, kc, :] for kc in range(HF)]
                for pc in range(4):
                    wv, wR_ = wload("w_fo", l, half * HF * 128, HF, pc * 256, 256)
                    for mo in range(2):
                        c = pc * 2 + mo
                        ps, pR = psum()
                        group(ps, pR, [wv[:, kc, mo * 128:(mo + 1) * 128] for kc in range(HF)], grhs, [wR_] + gR)
                        tt("dve", xT[:, c, :], xT[:, c, :], ps, ALU.add, [xR[c], pR], [xR[c]])
            if d0:
                dump("x2", xT.rearrange("p c t -> p (c t)"), xR, [128, KC * NT])
            norm()
            norm_apply(2 * DEPTH + l)
            for kc in range(2):
                S.dma("pool", pTb[:, kc, :], pT_d[l, kc * 128:(kc + 1) * 128, t0:t0 + NT], W=[pTbR])
            wpp, wppR = wload("w_pp", l, 0, 2, 0, D)
            for pc in range(2):
                wv, wR_ = wload("w_pg", l, 0, KC, pc * 512, 512)
                for mo in range(4):
                    c = pc * 4 + mo
                    j = c % 2
                    ps, pR = psum()
                    group(ps, pR, [wv[:, kc, mo * 128:(mo + 1) * 128] for kc in range(KC)], hrhs, [wR_] + hR)
                    ps2, pR2 = psum()
                    group(ps2, pR2, [wpp[:, kc, c * 128:(c + 1) * 128] for kc in range(2)], [pTb[:, kc, :] for kc in range(2)], [wppR, pTbR])
                    act(tmpA[:, j, :], ps, AF.Sigmoid, [pR], [tmpAR[j]])
                    tt("dve", tmpB[:, j, :], tmpA[:, j, :], ps2, ALU.mult, [tmpAR[j], pR2], [tmpBR[j]])
                    tt("pool", xT[:, c, :], xT[:, c, :], tmpB[:, j, :], ALU.add, [xR[c], tmpBR[j]], [xR[c]])
        norm()
        for c in range(KC):
            j = c % 2
            S.op("dve", lambda g: g.scalar_tensor_tensor(out=tmpB[:, j, :], in0=xT[:, c, :],
                                                         scalar=gains[:, 3 * DEPTH * KC + c:3 * DEPTH * KC + c + 1], in1=rstd,
                                                         op0=ALU.mult, op1=ALU.mult), [xR[c], rstdR, smallR], [tmpBR[j]])
            S.dma("sp", outT_d[c * 128:(c + 1) * 128, t0:t0 + NT], tmpB[:, j, :], R=[tmpBR[j]], W=[], semres=tmpBR[j])
    for r in tmpBR + list(dbg.values()):
        if r.semcnt:
            nc.sync.wait_ge(r.sem, r.semcnt)
    return nc, S


def t5_bucket_np(dist):
    exact = 16
    df = np.maximum(dist, 1).astype(np.float32)
    large = exact + (np.log(df / exact) / math.log(128 / exact) * (32 - exact)).astype(np.int32)
    large = np.minimum(large, 31)
    return np.where(dist < exact, dist, large)


def prep_shared(inp, DEPTH):
    f = np.float32
    w_in = np.asarray(inp["w_in"], f)
    u = w_in[:, :, 0:512]
    cb = w_in[:, :, 512:1024]; cc = w_in[:, :, 1024:1536]; cx = w_in[:, :, 1536:2048]
    q = w_in[:, :, 2048:2560]; k = w_in[:, :, 2560:2688]; v = w_in[:, :, 2688:2816]
    gates = w_in[:, :, 2816:]
    conv = np.concatenate([np.concatenate([cb[:, :, c * 128:(c + 1) * 128], cc[:, :, c * 128:(c + 1) * 128],
                                           cx[:, :, c * 128:(c + 1) * 128]], -1) for c in range(4)], -1)
    hperm = [h for c in range(4) for h in (c, c + 4)]
    qp = np.concatenate([q[:, :, h * 64:(h + 1) * 64] for h in hperm], -1)
    gp = np.concatenate([gates[:, :, r * D + c * 128: r * D + (c + 1) * 128] for c in range(8) for r in range(3)], -1)
    w_in_p = np.ascontiguousarray(np.concatenate([u, conv, qp, k, v, gp], -1))
    wbr = np.asarray(inp["w_branch"], f).copy()
    wbr[:, 2] = np.concatenate([wbr[:, 2, h * 64:(h + 1) * 64, :] for h in hperm], 1)
    w_br_p = np.ascontiguousarray(np.concatenate([wbr[:, r, :, c * 128:(c + 1) * 128] for c in range(8) for r in range(3)], -1))
    wfi = np.asarray(inp["w_ffn_in"], f)
    w_fi_p = np.ascontiguousarray(np.concatenate([wfi[:, :, o + j * 128: o + (j + 1) * 128] for j in range(FC) for o in (0, FF)], -1))
    sh = {"w_in": w_in_p, "w_glu": np.ascontiguousarray(inp["ssm_w_glu"], f), "w_br": w_br_p,
          "w_out": np.ascontiguousarray(inp["w_out"], f), "w_fi": w_fi_p,
          "w_fo": np.ascontiguousarray(inp["w_ffn_out"], f), "w_pg": np.ascontiguousarray(inp["w_ple_gate"], f),
          "w_pp": np.ascontiguousarray(inp["w_ple_proj"], f)}
    norms = [np.asarray(inp[n], f) for n in ("norm_mix", "norm_ffn", "norm_ple")]
    gl = []
    for nm in norms:
        for l in range(DEPTH):
            gl.append(nm[l].reshape(KC, 128).T)
    gl.append(np.asarray(inp["norm_final"], f).reshape(KC, 128).T)
    sh["gains"] = np.ascontiguousarray(np.concatenate(gl, 1))
    cw = np.asarray(inp["conv_w"], f)
    sh["convw"] = np.ascontiguousarray(cw.reshape(DEPTH, 3, 4, 128).transpose(3, 0, 2, 1).reshape(128, DEPTH * 12))
    sh["dskip"] = np.ascontiguousarray(np.asarray(inp["ssm_d"], f).reshape(DEPTH, 4, 128).transpose(2, 0, 1).reshape(128, DEPTH * 4))
    sh["sinkb"] = np.ascontiguousarray(np.broadcast_to(np.asarray(inp["attn_sinks"], f).reshape(1, DEPTH * 8), (128, DEPTH * 8)))
    rb = np.asarray(inp["rel_bias"], f)
    qi = np.arange(128)[:, None]; kj = np.arange(256)[None, :]
    dist = qi + 128 - kj
    band = (dist >= 0) & (dist < 128)
    bucket = t5_bucket_np(np.clip(dist, 0, 127))
    bias = rb[bucket]
    bm = np.where(band[:, :, None], bias, f(-30000.0)).astype(f)
    sh["biasmask"] = np.ascontiguousarray(bm.transpose(0, 2, 1).reshape(128, 8 * 256))
    sh["identf"] = np.eye(128, dtype=f)
    lre = np.asarray(inp["ssm_lambda_re"], f); lim = np.asarray(inp["ssm_lambda_im"], f)
    ldt = np.asarray(inp["ssm_log_dt"], f)
    ldt_b = np.broadcast_to(ldt[:, :, None], (DEPTH, G, NS))

    def playout(a):
        return a.reshape(DEPTH, 16, 2, NS).transpose(2, 3, 0, 1).reshape(128, DEPTH * 16)
    sh["lamP"] = np.ascontiguousarray(np.concatenate([playout(lre), playout(lim), playout(ldt_b)], 1))
    lr = np.stack([lre.reshape(DEPTH, G * NS), lim.reshape(DEPTH, G * NS), ldt_b.reshape(DEPTH, G * NS)], 1)
    sh["lamR"] = np.ascontiguousarray(np.broadcast_to(lr[:, :, None, :], (DEPTH, 3, 128, G * NS)))
    bpad = np.zeros((DEPTH, 2, 128, G, NS), f)
    cpad = np.zeros((DEPTH, 2, 2, NS, 16, 128), f)
    bsrc = [np.asarray(inp["ssm_b_re"], f), np.asarray(inp["ssm_b_im"], f)]
    csrc = [np.asarray(inp["ssm_c_re"], f), np.asarray(inp["ssm_c_im"], f)]
    for g in range(G):
        g8 = g % 8
        for c in range(2):
            bpad[:, c, g8 * 16:(g8 + 1) * 16, g, :] = bsrc[c][:, g].transpose(0, 2, 1)
            cpad[:, c, g % 2, :, g // 2, g8 * 16:(g8 + 1) * 16] = csrc[c][:, g].transpose(0, 2, 1)
    sh["bpad"] = np.ascontiguousarray(bpad.reshape(DEPTH, 2, 128, G * NS))
    sh["cpad"] = np.ascontiguousarray(cpad.reshape(DEPTH, 2, 128, 16 * 128))
    return sh


_CACHE = {}


def run(inp, T, DEPTH, ncores, debug=None):
    sh = prep_shared(inp, DEPTH)
    x = np.asarray(inp["x"], np.float32); p = np.asarray(inp["p"], np.float32)
    in_maps = []
    for b in range(ncores):
        m = dict(sh)
        m["xT"] = np.ascontiguousarray(x[b].T)
        m["pT"] = np.ascontiguousarray(p[:, b].transpose(0, 2, 1))
        in_maps.append(m)
    key = (T, DEPTH, tuple(sorted(debug)) if debug else None)
    if key not in _CACHE:
        _CACHE[key] = build(T, DEPTH, debug)
    nc, S = _CACHE[key]
    res = run_bass_kernel_spmd(nc, in_maps, core_ids=list(range(ncores)))
    out = np.stack([np.ascontiguousarray(r["outT"].T) for r in res.results], 0)
    return out, res


def kernel(**inputs):
    x = inputs["x"]
    B, T, _ = x.shape
    DEPTH = inputs["w_in"].shape[0]
    out, _ = run(inputs, T, DEPTH, B)
    return out.astype(np.float32)
```
